# Optimizing a Trainium2 kernel written in Bass

```python
import jax
import jax.numpy as jnp
from jax import lax
import numpy as np

D_MODEL = 1024
BATCH = 8
SEQ = 2048
DEPTH = 4

GRID_W = 64
CTX_LEN = 256
EPS = 1e-6
ROPE_THETA = 10000.0
Q_BLOCK = 128

A_HEADS = 8
A_NOPE = 64
A_ROPE = 32
A_V = 64
A_QK = A_NOPE + A_ROPE
KV_LORA = 256
Q_LORA = 768
A_WIDTH = A_HEADS * A_V

B_HEADS = 4
B_DK = 64
B_DV = 128
B_KW = B_HEADS * B_DK
B_WIDTH = B_HEADS * B_DV
GATE_RANK = 16
GATE_TAU = 16.0
CHUNK = 64

C_HEADS = 16
C_HD = 64
C_WIDTH = C_HEADS * C_HD
WIN_ROWS = 8
WIN_COLS = 16

EVEN_SPLITS = (Q_LORA, KV_LORA, A_ROPE, A_WIDTH, B_KW, B_KW, B_WIDTH, 2 * GATE_RANK, B_WIDTH)
EVEN_IN = sum(EVEN_SPLITS)
ODD_IN = 4 * C_WIDTH
N_EVEN = (DEPTH + 1) // 2
N_ODD = DEPTH // 2

kernel_name = "hybrid_mla_gla_natten_prefix_dit"


def rms_norm(x, gain):
    xf = x.astype(jnp.float32)
    y = xf * lax.rsqrt(jnp.mean(xf * xf, axis=-1, keepdims=True) + EPS)
    return (y * gain.astype(jnp.float32)).astype(x.dtype)


def split_cols(u, sizes):
    idx = np.cumsum(sizes)[:-1].tolist()
    return jnp.split(u, idx, axis=-1)


def modulation(cond, w, b):
    m = jnp.dot(jax.nn.silu(cond), w) + b
    return jnp.split(m[..., None, :], 3, axis=-1)


def axial_rope_tables(n):
    t = jnp.arange(n)
    row = (t // GRID_W).astype(jnp.float32)
    col = (t % GRID_W).astype(jnp.float32)
    d_ax = A_ROPE // 2
    inv = ROPE_THETA ** (-jnp.arange(0, d_ax, 2, dtype=jnp.float32) / d_ax)
    ang = jnp.stack([row[:, None] * inv, col[:, None] * inv], axis=1)
    return jnp.cos(ang), jnp.sin(ang)


def apply_axial_rope(x, cos, sin):
    shp = x.shape
    xr = x.astype(jnp.float32).reshape(shp[:-1] + (2, 2, shp[-1] // 4))
    x1, x2 = xr[..., 0, :], xr[..., 1, :]
    cs, sn = cos[None, :, None], sin[None, :, None]
    out = jnp.stack([x1 * cs - x2 * sn, x2 * cs + x1 * sn], axis=-2)
    return out.reshape(shp).astype(x.dtype)


def block_attention(q, k, v, scale):
    bsz, n, h, dq = q.shape
    nb = n // Q_BLOCK
    qb = q.reshape(bsz, nb, Q_BLOCK, h, dq).transpose(1, 0, 2, 3, 4)

    def one(qi):
        s = jnp.einsum('bqhd,bkhd->bhqk', qi, k, preferred_element_type=jnp.float32) * scale
        p = jax.nn.softmax(s, axis=-1).astype(v.dtype)
        return jnp.einsum('bhqk,bkhd->bqhd', p, v)

    o = lax.map(one, qb)
    return o.transpose(1, 0, 2, 3, 4).reshape(bsz, n, h, v.shape[-1])


def mla_heads(q_lat, kv_lat, k_rope, q_norm, w_uq, kv_norm, w_ukv, q_gain, k_gain, rope):
    bsz, n, _ = q_lat.shape
    q = jnp.dot(rms_norm(q_lat, q_norm), w_uq).reshape(bsz, n, A_HEADS, A_QK)
    kv = jnp.dot(rms_norm(kv_lat, kv_norm), w_ukv).reshape(bsz, n, A_HEADS, A_NOPE + A_V)
    k = jnp.concatenate(
        [kv[..., :A_NOPE], jnp.broadcast_to(k_rope[:, :, None, :], (bsz, n, A_HEADS, A_ROPE))], axis=-1)
    q = rms_norm(q, q_gain)
    k = rms_norm(k, k_gain)
    if rope is not None:
        cos, sin = rope
        q = jnp.concatenate([q[..., :A_NOPE], apply_axial_rope(q[..., A_NOPE:], cos, sin)], axis=-1)
        k = jnp.concatenate([k[..., :A_NOPE], apply_axial_rope(k[..., A_NOPE:], cos, sin)], axis=-1)
    return q, k, kv[..., A_NOPE:]


def gla_inputs(gq, gk, gv, g_lr, gate_w, gate_b):
    bsz, n, _ = gq.shape
    q = gq.reshape(bsz, n, B_HEADS, B_DK) * (B_DK ** -0.5)
    k = gk.reshape(bsz, n, B_HEADS, B_DK)
    v = gv.reshape(bsz, n, B_HEADS, B_DV)
    r = g_lr.reshape(bsz, n, 2, GATE_RANK)
    logits = jnp.einsum('bnzr,zrk->bnzk', r, gate_w) + gate_b
    g = (jax.nn.log_sigmoid(logits.astype(jnp.float32)) / GATE_TAU).reshape(bsz, n, 2, B_HEADS, B_DK)
    return q, k, v, g[:, :, 0], g[:, :, 1]


def gla_scan(q, k, v, g, s0):
    bsz, n, h, dk = q.shape
    dv = v.shape[-1]
    nc = n // CHUNK

    def to_chunks(t):
        return t.astype(jnp.float32).reshape(bsz, nc, CHUNK, h, t.shape[-1]).transpose(1, 0, 3, 2, 4)

    mask = jnp.tril(jnp.ones((CHUNK, CHUNK), dtype=bool))

    def step(state, inp):
        qi, ki, vi, gi = inp
        b = jnp.cumsum(gi, axis=-2)
        b_last = b[..., -1:, :]
        q_t = qi * jnp.exp(b)
        k_t = ki * jnp.exp(-b)
        att = jnp.where(mask, jnp.einsum('bhld,bhmd->bhlm', q_t, k_t), 0.0)
        o = jnp.einsum('bhld,bhde->bhle', q_t, state) + jnp.einsum('bhlm,bhme->bhle', att, vi)
        k_dec = ki * jnp.exp(b_last - b)
        state = state * jnp.exp(b_last)[..., 0, :, None] + jnp.einsum('bhld,bhle->bhde', k_dec, vi)
        return state, o

    s_final, o = lax.scan(step, s0, (to_chunks(q), to_chunks(k), to_chunks(v), to_chunks(g)))
    return o.transpose(1, 0, 3, 2, 4).reshape(bsz, n, h, dv), s_final


def gla_bidirectional(lat, ctx):
    ql, kl, vl, gfl, gbl = lat
    qc, kc, vc, gfc, gbc = ctx
    s0 = jnp.zeros((ql.shape[0], B_HEADS, B_DK, B_DV), jnp.float32)
    flip = lambda t: jnp.flip(t, axis=1)
    oc_f, sc_f = gla_scan(qc, kc, vc, gfc, s0)
    oc_b, sc_b = gla_scan(flip(qc), flip(kc), flip(vc), flip(gbc), s0)
    ol_f, _ = gla_scan(ql, kl, vl, gfl, sc_f)
    ol_b, _ = gla_scan(flip(ql), flip(kl), flip(vl), flip(gbl), sc_b)
    return ol_f + flip(ol_b), oc_f + flip(oc_b)


def even_mixer(h, hc, rope, w_in, q_norm, w_uq, kv_norm, w_ukv, q_gain, k_gain,
               gate_w, gate_b, gla_norm, w_out, update_ctx):
    lat = split_cols(jnp.dot(h, w_in), EVEN_SPLITS)
    cp = split_cols(jnp.dot(hc, w_in), EVEN_SPLITS)
    mla_p = (q_norm, w_uq, kv_norm, w_ukv, q_gain, k_gain)
    q_x, k_x, v_x = mla_heads(lat[0], lat[1], lat[2], *mla_p, rope)
    q_c, k_c, v_c = mla_heads(cp[0], cp[1], cp[2], *mla_p, None)
    scale = A_QK ** -0.5
    a_x = block_attention(q_x, jnp.concatenate([k_x, k_c], axis=1), jnp.concatenate([v_x, v_c], axis=1), scale)
    b_x, b_c = gla_bidirectional(gla_inputs(lat[4], lat[5], lat[6], lat[7], gate_w, gate_b),
                                 gla_inputs(cp[4], cp[5], cp[6], cp[7], gate_w, gate_b))
    gla_gain = gla_norm.reshape(B_HEADS, B_DV)

    def readout(a, b, za, zb):
        bsz, n = a.shape[:2]
        a = a.reshape(bsz, n, A_WIDTH) * jax.nn.silu(za)
        b = rms_norm(b, gla_gain).reshape(bsz, n, B_WIDTH).astype(zb.dtype) * jax.nn.silu(zb)
        return jnp.dot(jnp.concatenate([a, b], axis=-1), w_out)

    y = readout(a_x, b_x, lat[3], lat[8])
    yc = None
    if update_ctx:
        a_c = block_attention(q_c, k_c, v_c, scale)
        yc = readout(a_c, b_c, cp[3], cp[8])
    return y, yc


def neighborhood_attention(q, k, v, k_ctx, v_ctx, rpb):
    bsz, n, h, d = q.shape
    rows = n // GRID_W
    kr = min(WIN_ROWS, rows)
    kc = WIN_COLS
    scale = d ** -0.5
    kg = k.reshape(bsz, rows, GRID_W, h, d)
    vg = v.reshape(bsz, rows, GRID_W, h, d)
    qg = q.reshape(bsz, rows, GRID_W, h, d).transpose(1, 0, 2, 3, 4)
    col = jnp.arange(GRID_W)
    col_start = jnp.clip(col - kc // 2, 0, GRID_W - kc)
    col_mask = (col[None, :] >= col_start[:, None]) & (col[None, :] < col_start[:, None] + kc)
    col_idx = jnp.clip(col[None, :] - col[:, None] + WIN_COLS - 1, 0, 2 * WIN_COLS - 2)

    def one_row(args):
        qr, r = args
        r_start = jnp.clip(r - kr // 2, 0, rows - kr)
        kb = lax.dynamic_slice_in_dim(kg, r_start, kr, axis=1)
        vb = lax.dynamic_slice_in_dim(vg, r_start, kr, axis=1)
        row_idx = r_start + jnp.arange(kr) - r + WIN_ROWS - 1
        bias = rpb[:, row_idx][:, :, col_idx].transpose(0, 2, 1, 3)
        s_lat = jnp.einsum('bqhd,brkhd->bhqrk', qr, kb, preferred_element_type=jnp.float32) * scale
        s_lat = s_lat + bias.astype(jnp.float32)[None]
        s_lat = jnp.where(col_mask[None, None, :, None, :], s_lat, -jnp.inf).reshape(bsz, h, GRID_W, kr * GRID_W)
        s_ctx = jnp.einsum('bqhd,bchd->bhqc', qr, k_ctx, preferred_element_type=jnp.float32) * scale
        p = jax.nn.softmax(jnp.concatenate([s_lat, s_ctx], axis=-1), axis=-1).astype(v.dtype)
        o = jnp.einsum('bhqk,bkhd->bqhd', p[..., :kr * GRID_W], vb.reshape(bsz, kr * GRID_W, h, d))
        return o + jnp.einsum('bhqc,bchd->bqhd', p[..., kr * GRID_W:], v_ctx)

    o = lax.map(one_row, (qg, jnp.arange(rows)))
    return o.transpose(1, 0, 2, 3, 4).reshape(bsz, n, h, d)


def odd_mixer(h, hc, w_in, q_gain, k_gain, rpb, w_out, update_ctx):
    def heads(u):
        bsz, n, _ = u.shape
        q, k, v, z = jnp.split(u, 4, axis=-1)
        sh = lambda t: t.reshape(bsz, n, C_HEADS, C_HD)
        return rms_norm(sh(q), q_gain), rms_norm(sh(k), k_gain), sh(v), z

    q_x, k_x, v_x, z_x = heads(jnp.dot(h, w_in))
    q_c, k_c, v_c, z_c = heads(jnp.dot(hc, w_in))
    o = neighborhood_attention(q_x, k_x, v_x, k_c, v_c, rpb)
    y = jnp.dot(o.reshape(h.shape[0], h.shape[1], C_WIDTH) * jax.nn.silu(z_x), w_out)
    yc = None
    if update_ctx:
        o_c = block_attention(q_c, k_c, v_c, C_HD ** -0.5)
        yc = jnp.dot(o_c.reshape(hc.shape[0], hc.shape[1], C_WIDTH) * jax.nn.silu(z_c), w_out)
    return y, yc


def setup_inputs(seed: int = 0) -> dict:
    key = jax.random.key(seed)
    ks = jax.random.split(key, 23)
    D = D_MODEL

    def normal(i, shape, s=1.0):
        return s * jax.random.normal(ks[i], shape, jnp.float32)

    def gain(i, shape):
        return 1.0 + normal(i, shape, 0.05)

    return {
        "x": normal(0, (BATCH, SEQ, D)),
        "c": normal(1, (BATCH, D)),
        "ctx": normal(2, (BATCH, CTX_LEN, D)),
        "c_ctx": normal(3, (D,)),
        "norm_g": gain(4, (DEPTH, D)),
        "ada_w": normal(5, (DEPTH, D, 3 * D), 0.5 * D ** -0.5),
        "ada_b": normal(6, (DEPTH, 3 * D), 0.02),
        "ev_w_in": normal(7, (N_EVEN, D, EVEN_IN), D ** -0.5),
        "ev_q_norm": gain(8, (N_EVEN, Q_LORA)),
        "ev_w_uq": normal(9, (N_EVEN, Q_LORA, A_HEADS * A_QK), Q_LORA ** -0.5),
        "ev_kv_norm": gain(10, (N_EVEN, KV_LORA)),
        "ev_w_ukv": normal(11, (N_EVEN, KV_LORA, A_HEADS * (A_NOPE + A_V)), KV_LORA ** -0.5),
        "ev_q_gain": gain(12, (N_EVEN, A_QK)),
        "ev_k_gain": gain(13, (N_EVEN, A_QK)),
        "ev_gate_w": normal(14, (N_EVEN, 2, GATE_RANK, B_KW), GATE_RANK ** -0.5),
        "ev_gate_b": normal(15, (N_EVEN, 2, B_KW), 0.1),
        "ev_gla_norm": gain(16, (N_EVEN, B_WIDTH)),
        "ev_w_out": normal(17, (N_EVEN, A_WIDTH + B_WIDTH, D), (A_WIDTH + B_WIDTH) ** -0.5),
        "od_w_in": normal(18, (N_ODD, D, ODD_IN), D ** -0.5),
        "od_q_gain": gain(19, (N_ODD, C_HD)),
        "od_k_gain": gain(20, (N_ODD, C_HD)),
        "od_rpb": normal(21, (N_ODD, C_HEADS, 2 * WIN_ROWS - 1, 2 * WIN_COLS - 1), 0.1),
        "od_w_out": normal(22, (N_ODD, C_WIDTH, D), C_WIDTH ** -0.5),
    }


def reference(x, c, ctx, c_ctx, norm_g, ada_w, ada_b, ev_w_in, ev_q_norm, ev_w_uq, ev_kv_norm, ev_w_ukv,
              ev_q_gain, ev_k_gain, ev_gate_w, ev_gate_b, ev_gla_norm, ev_w_out,
              od_w_in, od_q_gain, od_k_gain, od_rpb, od_w_out):
    rope = axial_rope_tables(x.shape[1])
    cx = ctx
    for i in range(DEPTH):
        update_ctx = i < DEPTH - 1
        shift, scale, gate = modulation(c, ada_w[i], ada_b[i])
        c_shift, c_scale, c_gate = modulation(c_ctx, ada_w[i], ada_b[i])
        h = rms_norm(x, norm_g[i]) * (1 + scale) + shift
        hc = rms_norm(cx, norm_g[i]) * (1 + c_scale) + c_shift
        j = i // 2
        if i % 2 == 0:
            y, yc = even_mixer(h, hc, rope, ev_w_in[j], ev_q_norm[j], ev_w_uq[j], ev_kv_norm[j], ev_w_ukv[j],
                               ev_q_gain[j], ev_k_gain[j], ev_gate_w[j], ev_gate_b[j], ev_gla_norm[j],
                               ev_w_out[j], update_ctx)
        else:
            y, yc = odd_mixer(h, hc, od_w_in[j], od_q_gain[j], od_k_gain[j], od_rpb[j], od_w_out[j], update_ctx)
        x = x + gate * y
        if update_ctx:
            cx = cx + c_gate * yc
    return x
```

```python
import numpy as np
from contextlib import ExitStack
import concourse.bass as bass
import concourse.mybir as mybir
from concourse.bass_utils import run_bass_kernel_spmd

F32 = mybir.dt.float32
BF16 = mybir.dt.bfloat16
I32 = mybir.dt.int32
AF = mybir.ActivationFunctionType
ALU = mybir.AluOpType
AX = mybir.AxisListType


class _Op:
    __slots__ = ("eng", "fn", "waits", "signal", "sem", "val", "is_dma", "idx")


class Prog:
    ENGS = ("pe", "act", "dve", "pool", "sp")

    def __init__(self, nc, stack, same_sync=True, n_dma_sems=(40, 16, 16)):
        self.nc = nc
        self.same_sync = same_sync
        self.ops = {e: [] for e in self.ENGS}
        self.all_ops = []
        self.state = {}
        self.esem = {e: stack.enter_context(nc.semaphore("s_" + e)) for e in self.ENGS}
        self.dpool = {}
        for q, n in zip(("sp", "act", "pool"), n_dma_sems):
            self.dpool[q] = [[stack.enter_context(nc.semaphore("d_%s%d" % (q, i))), 0, None] for i in range(n)]
        self.dnext = {q: 0 for q in self.dpool}
        self.out_dmas = []

    def _add_wait(self, op, prod):
        if prod is None or prod is op:
            return
        if (not prod.is_dma) and (not op.is_dma) and prod.eng == op.eng:
            if op.eng == "pe" or not self.same_sync:
                return
        if not prod.is_dma:
            prod.signal = True
        if prod not in op.waits:
            op.waits.append(prod)

    def _record(self, op, rd, wr):
        for k in rd:
            st = self.state.get(k)
            if st is None:
                st = self.state[k] = [None, []]
            self._add_wait(op, st[0])
            if k in ("psT", "modps") or (isinstance(k, tuple) and k[0] == "ps"):
                for r in st[1]:
                    if r.eng != op.eng:
                        self._add_wait(op, r)
        for k in wr:
            st = self.state.get(k)
            if st is None:
                st = self.state[k] = [None, []]
            self._add_wait(op, st[0])
            for r in st[1]:
                self._add_wait(op, r)
        for k in rd:
            self.state[k][1].append(op)
        for k in wr:
            self.state[k][0] = op
            self.state[k][1] = []
        op.idx = len(self.all_ops)
        self.all_ops.append(op)
        self.ops[op.eng].append(op)

    def op(self, eng, fn, rd=(), wr=()):
        o = _Op()
        o.eng, o.fn, o.waits, o.signal, o.is_dma = eng, fn, [], False, False
        o.sem, o.val = None, 0
        self._record(o, rd, wr)
        return o

    def dma(self, out, in_, rd=(), wr=(), q="sp", is_output=False, **kw):
        o = _Op()
        o.eng, o.waits, o.signal, o.is_dma = q, [], True, True
        o.fn = lambda e: e.dma_start(out=out, in_=in_, **kw)
        pool = self.dpool[q]
        j = self.dnext[q]
        self.dnext[q] = (j + 1) % len(pool)
        slot = pool[j]
        if slot[2] is not None:
            o.waits.append(slot[2])
        slot[1] += 16
        slot[2] = o
        o.sem, o.val = slot[0], slot[1]
        self._record(o, rd, wr)
        if is_output:
            self.out_dmas.append(o)
        return o

    def barrier(self):
        lasts = []
        for e in self.ENGS:
            for o in reversed(self.ops[e]):
                if o.fn is not None and not o.is_dma:
                    lasts.append(o)
                    break
        dmas = [s[2] for q in self.dpool for s in self.dpool[q] if s[2] is not None]
        for e in ("pe", "act", "dve", "pool", "sp"):
            o = _Op()
            o.eng, o.fn, o.waits, o.signal, o.is_dma = e, None, [], False, False
            o.sem, o.val = None, 0
            for p in lasts + dmas:
                if p.is_dma or p.eng != e:
                    if not p.is_dma:
                        p.signal = True
                    o.waits.append(p)
            o.idx = len(self.all_ops)
            self.all_ops.append(o)
            self.ops[e].append(o)

    def emit(self):
        nc = self.nc
        fin = _Op()
        fin.eng, fin.fn, fin.waits, fin.signal, fin.is_dma = "sp", None, list(self.out_dmas), False, False
        fin.sem, fin.val = None, 0
        self.ops["sp"].append(fin)
        self.all_ops.append(fin)
        cnt = {e: 0 for e in self.ENGS}
        for o in self.all_ops:
            if o.is_dma:
                continue
            if o.signal:
                cnt[o.eng] += 1
                o.sem, o.val = self.esem[o.eng], cnt[o.eng]
        self.sig_counts = cnt

        def run(e, eng):
            waited = {}
            for o in self.ops[e]:
                for p in o.waits:
                    key = id(p.sem)
                    if waited.get(key, 0) >= p.val:
                        continue
                    eng.wait_ge(p.sem, p.val)
                    waited[key] = p.val
                if o.fn is None:
                    continue
                inst = o.fn(eng)
                if o.is_dma:
                    inst.then_inc(o.sem, 16)
                elif o.signal:
                    inst.then_inc(o.sem, 1)

        with nc.Block() as block:
            @block.tensor
            def _(eng):
                run("pe", eng)

            @block.scalar
            def _(eng):
                run("act", eng)

            @block.vector
            def _(eng):
                run("dve", eng)

            @block.gpsimd
            def _(eng):
                run("pool", eng)

            @block.sync
            def _(eng):
                run("sp", eng)


D_MODEL = 1024
NCTX = 256
NLAT = 2048
NT = NCTX + NLAT
NTILE = NT // 128
EPS = 1e-6
TB = [(0, 256)] + [(256 + 512 * i, 512) for i in range(4)]
EV_IN = 3136
A_SCALE = 96 ** -0.5
C_SCALE = 64 ** -0.5


def blk_of_tile(tt):
    return 0 if tt < 2 else 1 + (tt - 2) // 4


def view(ap, dims, off=0):
    return bass.AP(ap.tensor, ap.offset + off, [list(ap.ap[0])] + [list(d) for d in dims])


def lockstep(gens):
    gens = list(gens)
    while gens:
        nxt = []
        for g in gens:
            try:
                next(g)
                nxt.append(g)
            except StopIteration:
                pass
        gens = nxt


class Rot:
    def __init__(self, tiles, name):
        self.tiles, self.name, self.i = tiles, name, 0

    def next(self):
        j = self.i % len(self.tiles)
        self.i += 1
        return self.tiles[j], (self.name, j)


class Builder:
    def __init__(self, layers, debug_full=False, stop_after=None):
        self.stop_after = stop_after
        self.layers = layers
        self.debug_full = debug_full
        self.nc = bass.Bass("TRN2", target_bir_lowering=False)
        self.D = {}

    def din(self, name, shape, dt=F32):
        self.D[name] = self.nc.dram_tensor(name, list(shape), dt, kind="ExternalInput").ap()

    def sb(self, st, name, shape, dt):
        self._uid = getattr(self, "_uid", 0) + 1
        return st.enter_context(self.nc.sbuf_tensor("sb%d_%s" % (self._uid, name), list(shape), dt))

    def rot(self, st, name, shape, dt, n):
        return Rot([self.sb(st, "%s%d" % (name, i), shape, dt) for i in range(n)], name)

    def mm(self, out, lhsT, rhs, start, stop, rd, wr, **kw):
        self.P.op("pe", lambda e: e.matmul(out, lhsT=lhsT, rhs=rhs, start=start, stop=stop, **kw), rd=rd, wr=wr)

    def act(self, out, in_, func, rd, wr, scale=1.0, bias=None):
        if bias is None:
            self.P.op("act", lambda e: e.activation(out=out, in_=in_, func=func, scale=scale), rd=rd, wr=wr)
        else:
            self.P.op("act", lambda e: e.activation(out=out, in_=in_, func=func, scale=scale, bias=bias), rd=rd, wr=wr)

    def tt(self, eng, out, in0, in1, op, rd, wr):
        self.P.op(eng, lambda e: e.tensor_tensor(out=out, in0=in0, in1=in1, op=op), rd=rd, wr=wr)

    def stt(self, out, in0, scalar, in1, op0, op1, rd, wr):
        self.P.op("dve", lambda e: e.scalar_tensor_tensor(out=out, in0=in0, scalar=scalar, in1=in1, op0=op0, op1=op1),
                  rd=rd, wr=wr)

    def ts(self, eng, out, in0, s1, op0, rd, wr, s2=None, op1=None):
        if op1 is None:
            self.P.op(eng, lambda e: e.tensor_scalar(out=out, in0=in0, scalar1=s1, scalar2=None, op0=op0), rd=rd, wr=wr)
        else:
            self.P.op(eng, lambda e: e.tensor_scalar(out=out, in0=in0, scalar1=s1, scalar2=s2, op0=op0, op1=op1),
                      rd=rd, wr=wr)

    def rstd_from(self, dst, src, n_feat, rd, wr):
        self.act(dst, src, AF.Ln, rd=rd, wr=wr, scale=1.0 / n_feat, bias=EPS)
        self.act(dst, dst, AF.Exp, rd=wr, wr=wr, scale=-0.5)

    def proj(self, ps_ap, wfn, bi, pskey, wkeys, **kw):
        t0, n = TB[bi]
        for k in range(8):
            self.mm(ps_ap, wfn(k), self.hT[:, k, t0:t0 + n], k == 0, k == 7,
                    rd=[("hT", bi)] + list(wkeys), wr=[pskey], **kw)

    def build(self):
        nc, D = self.nc, self.D
        self.din("xT", [D_MODEL, NT])
        self.din("c2", [128, 8, 2])
        self.din("norm_gT", [128, 4, 8])
        self.din("ada_w", [4, D_MODEL, 3 * D_MODEL])
        self.din("ada_bT", [128, 4, 24])
        self.din("ev_w_in", [2, D_MODEL, EV_IN + 32])
        self.din("ev_q_normT", [128, 2, 6])
        self.din("ev_w_uq", [2, 768, 768 + 256])
        self.din("ev_kv_normT", [128, 2, 2])
        self.din("ev_w_ukv", [2, 256, 1024])
        self.din("ev_gains", [128, 2, 4])
        self.din("ev_gate_w", [2, 2, 16, 256])
        self.din("ev_gate_bT", [64, 2, 8])
        self.din("ev_gla_normT", [128, 2, 4])
        self.din("ev_w_out", [2, D_MODEL, D_MODEL])
        self.din("od_w_in", [2, D_MODEL, 4096])
        self.din("od_gains", [128, 2, 2])
        self.din("od_rpbF", [2, 16 * 15 * 31 + 128])
        self.din("od_w_out", [2, D_MODEL, D_MODEL])
        self.din("rope", [2, 128, NLAT])
        self.din("tri", [128, 2, 128])
        self.din("colmask", [128, 64])
        self.din("bd64", [128, 128])
        D["outT"] = nc.dram_tensor("outT", [D_MODEL, NLAT], F32, kind="ExternalOutput").ap()
        if self.debug_full:
            D["dbgT"] = nc.dram_tensor("dbgT", [D_MODEL, NT], F32, kind="ExternalOutput").ap()
        D["xr"] = nc.dram_tensor("xr", [D_MODEL, NT], F32, kind="Internal").ap()

        with ExitStack() as st:
            self.P = P = Prog(nc, st, same_sync=True)
            self.ps = [st.enter_context(nc.psum_tensor("ps%d" % i, [128, 512], F32)) for i in range(7)]
            self.psT = st.enter_context(nc.psum_tensor("psT", [128, 1024], BF16))
            self.ones32 = self.sb(st, "ones32", [128, 128], F32)
            self.identb = self.sb(st, "identb", [128, 128], BF16)
            self.tri = self.sb(st, "tri", [128, 2, 128], F32)
            self.bd64 = self.sb(st, "bd64", [128, 128], F32)
            self.c2 = self.sb(st, "c2", [128, 8, 2], F32)
            self.sc2 = self.sb(st, "sc2", [128, 8, 2], BF16)
            self.normg = self.sb(st, "normg", [128, 4, 8], F32)
            self.adab = self.sb(st, "adab", [128, 4, 24], F32)
            self.mod = self.sb(st, "mod", [128, 4, 24, 2], F32)
            self.Amod = self.sb(st, "Amod", [128, 4, 8, 2], F32)
            self.evg = self.sb(st, "evg", [128, 2, 4], F32)
            self.evqn = self.sb(st, "evqn", [128, 2, 6], F32)
            self.evkvn = self.sb(st, "evkvn", [128, 2, 2], F32)
            self.evgb = self.sb(st, "evgb", [64, 2, 8], F32)
            self.evgl = self.sb(st, "evgl", [128, 2, 4], F32)
            self.odg = self.sb(st, "odg", [128, 2, 2], F32)
            self.preamble(st)
            for li, l in enumerate(self.layers if self.stop_after != "pre" else []):
                src = D["xT"] if li == 0 else D["xr"]
                last = li == len(self.layers) - 1
                with ExitStack() as lst:
                    self.hT = self.sb(lst, "hT", [128, 8, NT], BF16)
                    self.norm_phase(l, src, self.layers[li + 1] if li + 1 < len(self.layers) else None)
                    if self.stop_after == "norm":
                        break
                    self.aT = self.sb(lst, "aT", [128, 4, NT], BF16)
                    if l % 2 == 0:
                        self.even_mla(l)
                        if self.stop_after in ("mla_k", "mla"):
                            break
                        self.bT = self.sb(lst, "bT", [128, 4, NT], BF16)
                        self.even_gla(l)
                        if self.stop_after == "gla":
                            break
                        wsrc = D["ev_w_out"][l // 2]
                    else:
                        self.bT = self.sb(lst, "bT", [128, 4, NT], BF16)
                        self.odd_mixer(l)
                        wsrc = D["od_w_out"][l // 2]
                    self.wout_phase(l, wsrc, src, last)
                    P.barrier()
            P.emit()
        return nc

    def preamble(self, st):
        P, D = self.P, self.D
        P.op("pool", lambda e: e.memset(self.ones32[:], 1.0), wr=["ones32"])
        P.op("pool", lambda e: e.memset(self.identb[:], 0.0), wr=["identb"])
        P.op("pool", lambda e: e.affine_select(out=self.identb[:], in_=self.identb[:], pattern=[[-1, 128]],
                                               compare_op=ALU.not_equal, fill=1.0, base=0, channel_multiplier=1),
             rd=["identb"], wr=["identb"])
        for tile, name in ((self.tri, "tri"), (self.bd64, "bd64"), (self.c2, "c2"), (self.normg, "norm_gT"),
                           (self.adab, "ada_bT"), (self.evg, "ev_gains"), (self.evqn, "ev_q_normT"),
                           (self.evkvn, "ev_kv_normT"), (self.evgb, "ev_gate_bT"), (self.evgl, "ev_gla_normT"),
                           (self.odg, "od_gains")):
            P.dma(tile[:], D[name], wr=[name])
        self.act(self.sc2[:], self.c2[:], AF.Silu, rd=["c2"], wr=["sc2"])
        P.op("dve", lambda e: e.tensor_scalar(out=self.evgb[:], in0=self.evgb[:], scalar1=-1.0, scalar2=None, op0=ALU.mult),
             rd=["ev_gate_bT"], wr=["ev_gate_bT"])
        with ExitStack() as pst:
            wrot = self.rot(pst, "adaw", [128, 8, 512], BF16, 2)
            for _ in self.mod_gen(self.layers[0], wrot, self.ps[0], "modps"):
                pass
            P.barrier()

    def mod_gen(self, l, wrot, bank, bkey):
        P, D = self.P, self.D
        modps = bank[:, 0:48].rearrange("p (j t) -> p j t", t=2)
        for g in range(6):
            wt, wk = wrot.next()
            P.dma(wt[:], D["ada_w"][l, :, g * 512:(g + 1) * 512].rearrange("(k p) n -> p k n", p=128),
                  wr=[wk], q="pool")
            for jj in range(4):
                j = g * 4 + jj
                for k in range(8):
                    self.mm(modps[:, j, :], wt[:, k, jj * 128:(jj + 1) * 128], self.sc2[:, k, :], k == 0, k == 7,
                            rd=[wk, "sc2"], wr=[bkey])
            yield
        ab = self.adab[:, l, :]
        self.tt("dve", self.mod[:, l, :, :], modps, view(ab, [[1, 24], [0, 2]]), ALU.add,
                rd=[bkey, "ada_bT"], wr=[("mod", l)])
        ng = self.normg[:, l, :]
        self.stt(self.Amod[:, l, :, :], self.mod[:, l, 8:16, :], 1.0, view(ng, [[1, 8], [0, 2]]), ALU.add, ALU.mult,
                 rd=[("mod", l), "norm_gT"], wr=[("Amod", l)])
        yield

    def norm_phase(self, l, src, l_next=None):
        P = self.P
        with ExitStack() as st:
            mg = None
            if l_next is not None:
                wrot = self.rot(st, "adaw", [128, 8, 512], BF16, 2)
                mg = self.mod_gen(l_next, wrot, self.ps[5], ("ps", 5))
            xrot = self.rot(st, "xt", [128, 8, 512], F32, 2)
            sqrot = self.rot(st, "sq", [128, 8, 512], F32, 1)
            rrot = self.rot(st, "rs", [128, 512], F32, 2)
            trot = self.rot(st, "tn", [128, 512], F32, 3)
            for bi, (t0, n) in enumerate(TB):
                ci = 1 if bi == 0 else 0
                xt, xk = xrot.next()
                sq, sk = sqrot.next()
                rs, rk = rrot.next()
                P.dma(xt[:, :, 0:n], src[:, t0:t0 + n].rearrange("(k p) n -> p k n", p=128), rd=[("xres", bi)], wr=[xk])
                pb = self.ps[bi % 4]
                pk = ("ps", bi % 4)
                for k in range(8):
                    self.act(sq[:, k, 0:n], xt[:, k, 0:n], AF.Square, rd=[xk], wr=[(sk, k)])
                    self.mm(pb[:, 0:n], self.ones32[:, :], sq[:, k, 0:n], k == 0, k == 7, rd=["ones32", (sk, k)], wr=[pk])
                self.act(rs[:, 0:n], pb[:, 0:n], AF.Ln, rd=[pk], wr=[rk], scale=1.0 / D_MODEL, bias=EPS)
                self.act(rs[:, 0:n], rs[:, 0:n], AF.Exp, rd=[rk], wr=[rk], scale=-0.5)
                for k in range(8):
                    tn, tk = trot.next()
                    self.stt(tn[:, 0:n], xt[:, k, 0:n], self.Amod[:, l, k, ci:ci + 1], rs[:, 0:n], ALU.mult, ALU.mult,
                             rd=[xk, rk, ("Amod", l)], wr=[tk])
                    self.act(self.hT[:, k, t0:t0 + n], tn[:, 0:n], AF.Identity, rd=[tk, ("mod", l)], wr=[("hT", bi)],
                             bias=self.mod[:, l, k, ci:ci + 1])
                if mg is not None:
                    next(mg, None)
            if mg is not None:
                for _ in mg:
                    pass
            P.barrier()

    def even_mla(self, l):
        P, D = self.P, self.D
        j = l // 2
        upd_ctx = l < 3
        win = D["ev_w_in"][j]
        gq = lambda a, b: self.evg[a:b, j, 0:1]
        gqr = lambda a, b: self.evg[a:b, j, 1:2]
        gk = lambda a, b: self.evg[a:b, j, 2:3]
        gkr = lambda a, b: self.evg[a:b, j, 3:4]
        with ExitStack() as st:
            kT = self.sb(st, "kT", [128, 8, NT], BF16)
            Vt = self.sb(st, "Vt", [128, NTILE, 8, 65], BF16)
            wq = self.sb(st, "wq", [128, 8, 768], BF16)
            wuq = self.sb(st, "wuq", [128, 6, 1024], BF16)
            wza = self.sb(st, "wza", [128, 8, 512], BF16)
            fr = self.rot(st, "f", [128, 512], F32, 8)
            rp = self.rot(st, "ropeb", [128, 2, 512], F32, 2)
            P.op("pool", lambda e: e.memset(Vt[:, :, :, 64:65], 1.0), wr=[("V1",)])
            P.dma(wq[:], win[:, 0:768].rearrange("(k p) n -> p k n", p=128), wr=["wq"], q="pool")
            P.dma(wza[:], win[:, 1056:1568].rearrange("(k p) n -> p k n", p=128), wr=["wza"], q="pool")

            def load_rope(bi):
                t0, n = TB[bi]
                rb, rbk = rp.next()
                P.dma(rb[64:96, :, :], D["rope"][:, 64:96, t0 - NCTX:t0 - NCTX + n].rearrange("c p n -> p c n"), wr=[rbk])
                return rb, rbk

            with ExitStack() as ks:
                wkv = self.sb(ks, "wkv", [128, 8, 320], BF16)
                wukv = self.sb(ks, "wukv", [128, 2, 1024], BF16)
                stg = self.rot(ks, "stg", [128, 1024], F32, 2)
                kvlr = self.rot(ks, "kvl", [128, 2, 512], BF16, 2)
                sqkr = self.rot(ks, "sqk", [128, 2, 512], F32, 1)
                kst = self.rot(ks, "kst", [128, 512], F32, 6)
                rsv = self.rot(ks, "rsv", [128, 4], F32, 2)
                sqrz = self.sb(ks, "sqrz", [128, 512], F32)
                P.op("pool", lambda e: e.memset(sqrz[:], 0.0), wr=["sqrz"])
                P.dma(wkv[:, :, 0:288], win[:, 768:1056].rearrange("(k p) n -> p k n", p=128), wr=["wkv_a"], q="pool")
                P.dma(wkv[:, :, 288:320], win[:, 3136:3168].rearrange("(k p) n -> p k n", p=128), wr=["wkv_b"], q="pool")
                for c in range(2):
                    sg, sgk = stg.next()
                    P.dma(sg[:, :], D["ev_w_ukv"][j, c * 128:(c + 1) * 128, :], wr=[sgk])
                    self.ts("dve", wukv[:, c, :], sg[:, :], self.evkvn[:, j, c:c + 1], ALU.mult,
                            rd=[sgk, "ev_kv_normT"], wr=[("wukv", c)])
                for c in range(6):
                    sg, sgk = stg.next()
                    P.dma(sg[:, :], D["ev_w_uq"][j, c * 128:(c + 1) * 128, :], wr=[sgk])
                    self.ts("dve", wuq[:, c, :], sg[:, :], self.evqn[:, j, c:c + 1], ALU.mult,
                            rd=[sgk, "ev_q_normT"], wr=[("wuq", c)])
                wukv_keys = [("wukv", 0), ("wukv", 1)]
                for bi, (t0, n) in enumerate(TB):
                    lat = bi > 0
                    nt = n // 128
                    kvl, kvk = kvlr.next()
                    sqk, sqkk = sqkr.next()
                    for c in range(2):
                        pb = self.ps[c]
                        self.proj(pb[:, 0:n], lambda k, c=c: wkv[:, k, c * 128:(c + 1) * 128], bi, ("ps", c), ["wkv_a"])
                        self.act(kvl[:, c, 0:n], pb[:, 0:n], AF.Copy, rd=[("ps", c)], wr=[(kvk, c)])
                        self.act(sqk[:, c, 0:n], pb[:, 0:n], AF.Square, rd=[("ps", c)], wr=[(sqkk, c)])
                    for c in range(2):
                        self.mm(self.ps[2][:, 0:n], self.ones32[:, :], sqk[:, c, 0:n], c == 0, c == 1,
                                rd=["ones32", (sqkk, c)], wr=[("ps", 2)])
                    ak_, akk = kst.next()
                    svk, svkk = kst.next()
                    ck, ckk = kst.next()
                    self.act(ak_[:, 0:n], self.ps[2][:, 0:n], AF.Ln, rd=[("ps", 2)], wr=[akk], scale=1.0 / 256, bias=EPS)
                    self.act(svk[:, 0:n], ak_[:, 0:n], AF.Exp, rd=[akk], wr=[svkk], scale=0.5)
                    self.ts("dve", ck[:, 0:n], self.ps[2][:, 0:n], 1.0 / 256, ALU.mult, rd=[("ps", 2)], wr=[ckk], s2=EPS, op1=ALU.add)
                    rv, rvk = rsv.next()
                    for ti in range(nt):
                        for c in range(2):
                            self.mm(self.ps[3][:, ti:ti + 1], sqk[:, c, ti * 128:(ti + 1) * 128], self.ones32[:, 0:1],
                                    c == 0, c == 1, rd=["ones32", (sqkk, c)], wr=[("ps", 3)])
                    self.act(rv[:, 0:nt], self.ps[3][:, 0:nt], AF.Ln, rd=[("ps", 3)], wr=[rvk], scale=1.0 / 256, bias=EPS)
                    self.act(rv[:, 0:nt], rv[:, 0:nt], AF.Exp, rd=[rvk], wr=[rvk], scale=-0.5)
                    self.proj(self.ps[4][64:96, 0:n], lambda k: wkv[:, k, 256:288], bi, ("ps", 4), ["wkv_a"])
                    if lat:
                        self.proj(self.ps[5][64:96, 0:n], lambda k: wkv[:, k, 288:320], bi, ("ps", 5), ["wkv_b"])
                        rb, rbk = load_rope(bi)
                    sqr, sqrk = sqrz, "sqrz"
                    self.act(sqr[64:96, 0:n], self.ps[4][64:96, 0:n], AF.Square, rd=[("ps", 4)], wr=[sqrk])
                    self.mm(self.ps[6][:, 0:n], self.ones32[:, :], sqr[:, 0:n], True, True,
                            rd=["ones32", sqrk], wr=[("ps", 6)])
                    ssr, ssrk = kst.next()
                    self.ts("dve", ssr[:, 0:n], self.ps[6][:, 0:n], 1.0 / 96, ALU.mult, rd=[("ps", 6)], wr=[ssrk], s2=EPS, op1=ALU.add)
                    self.tt("dve", ck[:, 0:n], ck[:, 0:n], ssr[:, 0:n], ALU.mult, rd=[ckk, ssrk], wr=[ckk])
                    R, Rk = kst.next()
                    if lat:
                        t1, t1k = fr.next()
                        t2, t2k = fr.next()
                        self.stt(t1[64:96, 0:n], self.ps[4][64:96, 0:n], gk(64, 96), rb[64:96, 0, 0:n], ALU.mult, ALU.mult,
                                 rd=[("ps", 4), rbk, "ev_gains"], wr=[t1k])
                        self.stt(t2[64:96, 0:n], self.ps[5][64:96, 0:n], gkr(64, 96), rb[64:96, 1, 0:n], ALU.mult, ALU.mult,
                                 rd=[("ps", 5), rbk, "ev_gains"], wr=[t2k])
                        self.tt("dve", R[64:96, 0:n], t1[64:96, 0:n], t2[64:96, 0:n], ALU.add, rd=[t1k, t2k], wr=[Rk])
                    else:
                        self.act(R[64:96, 0:n], self.ps[4][64:96, 0:n], AF.Copy, rd=[("ps", 4), "ev_gains"], wr=[Rk],
                                 scale=gk(64, 96))
                    self.tt("dve", R[64:96, 0:n], R[64:96, 0:n], svk[64:96, 0:n], ALU.mult, rd=[Rk, svkk], wr=[Rk])
                    for ti in range(nt):
                        tt_ = t0 // 128 + ti
                        pv = self.ps[5 + (ti % 2)] if not lat else self.ps[3 + 3 * (ti % 2)]
                        pvk = ("ps", 5 + (ti % 2)) if not lat else ("ps", 3 + 3 * (ti % 2))
                        for c in range(2):
                            vcols = wukv[:, c, :].rearrange("p (h e) -> p h e", e=128)[:, :, 64:128]
                            self.mm(pv[:, :], kvl[:, c, ti * 128:(ti + 1) * 128], vcols, c == 0, c == 1,
                                    rd=[(kvk, c)] + wukv_keys, wr=[pvk])
                        self.act(Vt[:, tt_, :, 0:64], pv[:, :].rearrange("p (h e) -> p h e", e=64), AF.Copy,
                                 rd=[pvk, rvk], wr=[("V", tt_)], scale=rv[:, ti:ti + 1])
                    def khead(h):
                        pn = self.ps[h % 2]
                        pnk = ("ps", h % 2)
                        bs = 2 if h % 2 == 0 else 5
                        for c in range(2):
                            self.mm(pn[0:64, 0:n], wukv[:, c, h * 128:h * 128 + 64], kvl[:, c, 0:n], c == 0, c == 1,
                                    rd=[(kvk, c)] + wukv_keys, wr=[pnk])
                        yield
                        sqn, sqnk = fr.next()
                        self.act(sqn[0:64, 0:n], pn[0:64, 0:n], AF.Square, rd=[pnk], wr=[sqnk])
                        yield
                        self.mm(self.ps[bs][:, 0:n], self.ones32[0:64, :], sqn[0:64, 0:n], True, True,
                                rd=["ones32", sqnk], wr=[("ps", bs)])
                        yield
                        tb_, tbk = fr.next()
                        self.stt(tb_[:, 0:n], self.ps[bs][:, 0:n], 1.0 / 96, ck[:, 0:n], ALU.mult, ALU.add, rd=[("ps", bs), ckk], wr=[tbk])
                        yield
                        self.act(tb_[:, 0:n], tb_[:, 0:n], AF.Ln, rd=[tbk], wr=[tbk])
                        self.act(tb_[:, 0:n], tb_[:, 0:n], AF.Exp, rd=[tbk], wr=[tbk], scale=-0.5)
                        yield
                        self.stt(kT[0:64, h, t0:t0 + n], pn[0:64, 0:n], gk(0, 64), tb_[0:64, 0:n], ALU.mult, ALU.mult,
                                 rd=[pnk, tbk, "ev_gains"], wr=[("kT", bi, h, 0)])
                        self.tt("dve", kT[64:96, h, t0:t0 + n], R[64:96, 0:n], tb_[64:96, 0:n], ALU.mult,
                                rd=[Rk, tbk], wr=[("kT", bi, h, 1)])
                        yield

                    for h0 in range(0, 8, 2):
                        lockstep([khead(h0), khead(h0 + 1)])
                P.barrier()

            if self.stop_after == "mla_k":
                return
            with ExitStack() as qs:
                qlb = self.sb(qs, "qlb", [128, 6, 512], BF16)
                qT = self.sb(qs, "qT", [128, 8, 512], BF16)
                rqt = self.sb(qs, "rqt", [128, 512], F32)
                eqt = self.sb(qs, "eqt", [128, 512], F32)
                ptr = self.rot(qs, "pt", [128, 512], BF16, 5)
                sz8 = self.sb(qs, "sz8", [64, 8, 512], F32)
                for bi, (t0, n) in enumerate(TB):
                    lat = bi > 0
                    if not lat and not upd_ctx:
                        continue
                    if lat:
                        rb, rbk = load_rope(bi)
                    for jq in range(6):
                        pb = self.ps[jq % 2]
                        pk = ("ps", jq % 2)
                        self.proj(pb[:, 0:n], lambda k, jq=jq: wq[:, k, jq * 128:(jq + 1) * 128], bi, pk, ["wq"])
                        self.act(qlb[:, jq, 0:n], pb[:, 0:n], AF.Copy, rd=[pk], wr=[("qlb", jq)])
                        sq, sqk_ = fr.next()
                        self.act(sq[:, 0:n], pb[:, 0:n], AF.Square, rd=[pk], wr=[sqk_])
                        self.mm(self.ps[2][:, 0:n], self.ones32[:, :], sq[:, 0:n], jq == 0, jq == 5,
                                rd=["ones32", sqk_], wr=[("ps", 2)])
                    self.ts("dve", eqt[:, 0:n], self.ps[2][:, 0:n], EPS / 768, ALU.mult, rd=[("ps", 2)], wr=["eqt"],
                            s2=EPS * EPS, op1=ALU.add)
                    qlb_keys = [("qlb", jq) for jq in range(6)]
                    wuq_keys = [("wuq", c) for c in range(6)]
                    def qhead(h):
                        b3, b4, b5 = (3, 4, 5) if h % 2 == 0 else (6, 0, 1)
                        p3, p4, p5 = self.ps[b3], self.ps[b4], self.ps[b5]
                        for jq in range(6):
                            self.mm(p3[0:96, 0:n], wuq[:, jq, h * 96:(h + 1) * 96], qlb[:, jq, 0:n], jq == 0, jq == 5,
                                    rd=qlb_keys + wuq_keys, wr=[("ps", b3)])
                        if lat:
                            for jq in range(6):
                                self.mm(p4[64:96, 0:n], wuq[:, jq, 768 + h * 32:768 + (h + 1) * 32], qlb[:, jq, 0:n],
                                        jq == 0, jq == 5, rd=qlb_keys + wuq_keys, wr=[("ps", b4)])
                        yield
                        sqh, sqhk = fr.next()
                        self.act(sqh[0:96, 0:n], p3[0:96, 0:n], AF.Square, rd=[("ps", b3)], wr=[sqhk])
                        yield
                        self.mm(p5[:, 0:n], self.ones32[0:96, :], sqh[0:96, 0:n], True, True, rd=["ones32", sqhk], wr=[("ps", b5)])
                        yield
                        tb_, tbk = fr.next()
                        self.stt(tb_[:, 0:n], p5[:, 0:n], 1.0 / 96, eqt[:, 0:n], ALU.mult, ALU.add, rd=[("ps", b5), "eqt"], wr=[tbk])
                        yield
                        self.act(tb_[:, 0:n], tb_[:, 0:n], AF.Ln, rd=[tbk], wr=[tbk])
                        self.act(tb_[:, 0:n], tb_[:, 0:n], AF.Exp, rd=[tbk], wr=[tbk], scale=-0.5)
                        yield
                        if lat:
                            self.stt(qT[0:64, h, 0:n], p3[0:64, 0:n], gq(0, 64), tb_[0:64, 0:n], ALU.mult, ALU.mult,
                                     rd=[("ps", b3), tbk, "ev_gains"], wr=[("qT", h, 0)])
                            t1, t1k = fr.next()
                            t2, t2k = fr.next()
                            self.stt(t1[64:96, 0:n], p3[64:96, 0:n], gq(64, 96), rb[64:96, 0, 0:n], ALU.mult, ALU.mult,
                                     rd=[("ps", b3), rbk, "ev_gains"], wr=[t1k])
                            self.stt(t2[64:96, 0:n], p4[64:96, 0:n], gqr(64, 96), rb[64:96, 1, 0:n], ALU.mult, ALU.mult,
                                     rd=[("ps", b4), rbk, "ev_gains"], wr=[t2k])
                            self.tt("dve", t1[64:96, 0:n], t1[64:96, 0:n], t2[64:96, 0:n], ALU.add, rd=[t1k, t2k], wr=[t1k])
                            self.tt("dve", qT[64:96, h, 0:n], t1[64:96, 0:n], tb_[64:96, 0:n], ALU.mult,
                                    rd=[t1k, tbk], wr=[("qT", h, 1)])
                        else:
                            self.stt(qT[0:96, h, 0:n], p3[0:96, 0:n], gq(0, 96), tb_[0:96, 0:n], ALU.mult, ALU.mult,
                                     rd=[("ps", b3), tbk, "ev_gains"], wr=[("qT", h, 0), ("qT", h, 1)])
                        yield

                    for h0 in range(0, 8, 2):
                        lockstep([qhead(h0), qhead(h0 + 1)])
                    for h in range(8):
                        bz = 6 if h % 2 == 0 else 3
                        self.proj(self.ps[bz][0:64, 0:n], lambda k, h=h: wza[:, k, h * 64:(h + 1) * 64], bi, ("ps", bz), ["wza"])
                        self.act(sz8[:, h, 0:n], self.ps[bz][0:64, 0:n], AF.Silu, rd=[("ps", bz)], wr=[("sz8", h)])
                    keys = [0, 1] if not lat else list(range(NTILE))
                    for h0 in range(0, 8, 2):
                        streams = []
                        for h in (h0, h0 + 1):
                            streams.append(dict(
                                po=self.ps[4 + (h % 2)], bo=4 + (h % 2),
                                kfn=lambda kt, h=h: kT[0:96, h, kt * 128:(kt + 1) * 128],
                                kkeys=lambda kt, h=h: [("kT", blk_of_tile(kt), h, 0), ("kT", blk_of_tile(kt), h, 1)],
                                q_ap=qT[0:96, h, 0:n], qkeys=[("qT", h, 0), ("qT", h, 1)],
                                vfn=lambda kt, h=h: Vt[:, kt, h, :], vkeys=lambda kt: [("V", kt), ("V1",)], post=None))
                        self.attn_multi(streams, keys, n, A_SCALE, ptr)
                        for h in (h0, h0 + 1):
                            dst = self.aT[(h % 2) * 64:(h % 2) * 64 + 64, h // 2, t0:t0 + n]
                            self.attn_finish(self.ps[4 + (h % 2)], 4 + (h % 2), n, sz8[:, h, 0:n], ("sz8", h), dst,
                                             [("aT", bi, h)], fr, bbc=(0 if h % 2 == 0 else 1))
                P.barrier()

    def attn_core(self, po, bo, keys, n, kfn, kkeys, q_ap, qkeys, vfn, vkeys, scale, ptr, post):
        self.attn_multi([dict(po=po, bo=bo, kfn=kfn, kkeys=kkeys, q_ap=q_ap, qkeys=qkeys, vfn=vfn, vkeys=vkeys, post=post)],
                        keys, n, scale, ptr)

    def attn_multi(self, streams, keys, n, scale, ptr):
        nk = len(keys)
        ns = len(streams)
        SB = [0, 1, 2, 6] if ns == 1 else [0, 1, 2, 6, 3, 7]
        LA = 3 if ns == 1 else 2
        seq = [(i, si) for i in range(nk) for si in range(ns)]
        nseq = len(seq)

        def bank(j):
            b = SB[j % len(SB)]
            if b == 7:
                return self.psT[:, :].bitcast(F32), "psT"
            return self.ps[b], ("ps", b)

        def s_mm(j):
            i, si = seq[j]
            st = streams[si]
            kt = keys[i]
            pb, pk = bank(j)
            self.mm(pb[:, 0:n], st["kfn"](kt), st["q_ap"], True, True, rd=list(st["kkeys"](kt)) + list(st["qkeys"]), wr=[pk])

        ahead = LA * ns
        for j in range(min(ahead, nseq)):
            s_mm(j)
        for j in range(nseq):
            if j + ahead < nseq:
                s_mm(j + ahead)
            i, si = seq[j]
            st = streams[si]
            kt = keys[i]
            pb, pk = bank(j)
            if st["post"] is None:
                pt, ptk = ptr.next()
                self.act(pt[:, 0:n], pb[:, 0:n], AF.Exp, rd=[pk], wr=[ptk], scale=scale)
            else:
                pt, ptk = st["post"](i, kt, pb, pk)
            self.mm(st["po"][0:65, 0:n], st["vfn"](kt), pt[:, 0:n], i == 0, i == nk - 1,
                    rd=list(st["vkeys"](kt)) + [ptk], wr=[("ps", st["bo"])])

    def attn_finish(self, po, bo, n, gate_ap, gate_key, dst, dst_keys, fr, bbc=3):
        rd_, rdk = fr.next()
        self.act(rd_[64:65, 0:n], po[64:65, 0:n], AF.Ln, rd=[("ps", bo)], wr=[rdk])
        self.act(rd_[64:65, 0:n], rd_[64:65, 0:n], AF.Exp, rd=[rdk], wr=[rdk], scale=-1.0)
        pbc = self.ps[bbc]
        self.mm(pbc[0:64, 0:n], self.ones32[64:65, 0:64], rd_[64:65, 0:n], True, True, rd=["ones32", rdk], wr=[("ps", bbc)])
        tm, tmk = fr.next()
        self.tt("dve", tm[0:64, 0:n], po[0:64, 0:n], gate_ap, ALU.mult, rd=[("ps", bo), gate_key], wr=[tmk])
        self.tt("dve", dst, tm[0:64, 0:n], pbc[0:64, 0:n], ALU.mult, rd=[tmk, ("ps", bbc)], wr=dst_keys)

    def even_gla(self, l):
        P, D = self.P, self.D
        j = l // 2
        upd_ctx = l < 3
        win = D["ev_w_in"][j]
        with ExitStack() as st:
            wg = self.sb(st, "wg", [128, 8, 32], BF16)
            gw = self.sb(st, "gw", [16, 2, 256], BF16)
            rT = self.sb(st, "rT", [16, 2, NT], BF16)
            rmask = self.sb(st, "rmask", [64, NT], F32)
            whr = self.rot(st, "wh", [128, 8, 384], BF16, 2)
            qTh = self.sb(st, "qTh", [64, NT], BF16)
            kTh = self.sb(st, "kTh", [64, NT], BF16)
            vh = self.sb(st, "vh", [128, NTILE, 128], BF16)
            oT = self.sb(st, "oT", [128, NT], F32)
            spT = self.sb(st, "spT", [64, NT], F32)
            csT = self.sb(st, "csT", [64, NT], F32)
            qt = self.sb(st, "qt", [64, NT], BF16)
            kt = self.sb(st, "kt", [64, NT], BF16)
            kttok = self.sb(st, "kttok", [128, NTILE, 64], BF16)
            dec = self.sb(st, "dec", [64, NTILE], F32)
            dord = self.sb(st, "dord", [64, NTILE], F32)
            decs = self.sb(st, "decs", [64, 128, NTILE], F32)
            kvd = self.sb(st, "kvd", [64, 128, NTILE], F32)
            stt_ = self.sb(st, "stt", [64, 128, NTILE], F32)
            Sball = self.sb(st, "Sball", [64, NTILE, 128], BF16)
            atall = self.sb(st, "atall", [128, NTILE, 128], BF16)
            szb = self.sb(st, "szb", [128, NT], BF16)
            fr = self.rot(st, "g", [128, 512], F32, 3)
            P.dma(wg[:], win[:, 2592:2624].rearrange("(k p) n -> p k n", p=128), wr=["wg"], q="pool")
            P.dma(gw[:], D["ev_gate_w"][j].rearrange("z r k -> r z k"), wr=["gw"], q="pool")
            P.op("dve", lambda e: e.memset(rmask[:], 1.0), wr=["rmask"])
            rmv = rmask[:].rearrange("p (c l) -> p c l", l=128)
            P.op("dve", lambda e: e.memset(rmv[:, :, 0:1], 0.0), rd=["rmask"], wr=["rmask"])
            for bi, (t0, n) in enumerate(TB):
                for z in range(2):
                    self.proj(self.ps[z][0:16, 0:n], lambda k, z=z: wg[:, k, z * 16:(z + 1) * 16], bi, ("ps", z), ["wg"])
                    self.act(rT[0:16, z, t0:t0 + n], self.ps[z][0:16, 0:n], AF.Copy, rd=[("ps", z)], wr=[("rT", bi, z)])
            rT_keys = lambda z: [("rT", bi, z) for bi in range(len(TB))]
            for h in range(4):
                wh, whk = whr.next()
                for (c0, w, d0) in ((1568 + 64 * h, 64, 0), (1824 + 64 * h, 64, 64), (2080 + 128 * h, 128, 128),
                                    (2624 + 128 * h, 128, 256)):
                    P.dma(wh[:, :, d0:d0 + w], win[:, c0:c0 + w].rearrange("(k p) n -> p k n", p=128), wr=[(whk, d0)], q="pool")
                for bi, (t0, n) in enumerate(TB):
                    bq, bk = 2 * (bi % 3), 2 * (bi % 3) + 1
                    self.proj(self.ps[bq][0:64, 0:n], lambda k: wh[:, k, 0:64], bi, ("ps", bq), [(whk, 0)])
                    self.act(qTh[:, t0:t0 + n], self.ps[bq][0:64, 0:n], AF.Copy, rd=[("ps", bq)], wr=[("qTh", bi)], scale=0.125)
                    self.proj(self.ps[bk][0:64, 0:n], lambda k: wh[:, k, 64:128], bi, ("ps", bk), [(whk, 64)])
                    self.act(kTh[:, t0:t0 + n], self.ps[bk][0:64, 0:n], AF.Copy, rd=[("ps", bk)], wr=[("kTh", bi)])
                for tt_ in range(NTILE):
                    b = tt_ % 4
                    for k in range(8):
                        self.mm(self.ps[b][:, 0:128], self.hT[:, k, tt_ * 128:(tt_ + 1) * 128], wh[:, k, 128:256], k == 0, k == 7,
                                rd=[("hT", blk_of_tile(tt_)), (whk, 128)], wr=[("ps", b)])
                    self.act(vh[:, tt_, :], self.ps[b][:, 0:128], AF.Copy, rd=[("ps", b)], wr=[("vh", tt_)])
                for bi, (t0, n) in enumerate(TB):
                    if bi == 0 and not upd_ctx:
                        continue
                    bz = bi % 2
                    self.proj(self.ps[bz][:, 0:n], lambda k: wh[:, k, 256:384], bi, ("ps", bz), [(whk, 256)])
                    self.act(szb[:, t0:t0 + n], self.ps[bz][:, 0:n], AF.Silu, rd=[("ps", bz)], wr=[("szb", bi)])
                allq = [("qTh", bi) for bi in range(len(TB))]
                allk = [("kTh", bi) for bi in range(len(TB))]
                for z in range(2):
                    eT = spT
                    for bi, (t0, n) in enumerate(TB):
                        b = 4 + bi % 2
                        self.mm(self.ps[b][0:64, 0:n], gw[0:16, z, h * 64:(h + 1) * 64], rT[0:16, z, t0:t0 + n], True, True,
                                rd=["gw", ("rT", bi, z)], wr=[("ps", b)])
                        self.act(spT[:, t0:t0 + n], self.ps[b][0:64, 0:n], AF.Exp, rd=[("ps", b), "ev_gate_bT"], wr=["spT"],
                                 scale=-1.0, bias=self.evgb[:, j, z * 4 + h:z * 4 + h + 1])
                        self.act(spT[:, t0:t0 + n], spT[:, t0:t0 + n], AF.Ln, rd=["spT"], wr=["spT"], bias=1.0)
                    if z == 0:
                        P.op("dve", lambda e: e.tensor_tensor_scan(out=csT[:, :], data0=rmask[:, :], data1=spT[:, :], initial=0.0,
                                                                   op0=ALU.mult, op1=ALU.add),
                             rd=["rmask", "spT"], wr=["csT"])
                    else:
                        rev = lambda t: view(t[:, NT - 1:NT], [[-1, NT]])
                        P.op("dve", lambda e: e.tensor_tensor_scan(out=rev(csT), data0=rmask[:, :], data1=rev(spT), initial=0.0,
                                                                   op0=ALU.mult, op1=ALU.add),
                             rd=["rmask", "spT"], wr=["csT"])
                    self.act(eT[:, :], csT[:, :], AF.Exp, rd=["csT"], wr=["spT"], scale=-1.0 / 16)
                    self.tt("dve", qt[:, :], qTh[:, :], eT[:, :], ALU.mult, rd=allq + ["spT"], wr=["qt"])
                    self.act(eT[:, :], csT[:, :], AF.Exp, rd=["csT"], wr=["spT"], scale=1.0 / 16)
                    self.tt("dve", kt[:, :], kTh[:, :], eT[:, :], ALU.mult, rd=allk + ["spT"], wr=["kt"])
                    order = list(range(NTILE)) if z == 0 else [1, 0] + list(range(NTILE - 1, 1, -1))
                    pos = {c: i for i, c in enumerate(order)}
                    endcol = 127 if z == 0 else 0
                    self.act(dec[:, :], view(csT[:, endcol:endcol + 1], [[128, NTILE]]), AF.Exp, rd=["csT"], wr=["dec"],
                             scale=-1.0 / 16)
                    P.op("dve", lambda e: e.memset(dord[:, 0:1], 0.0), rd=["dec"], wr=["dord"])
                    if z == 0:
                        P.op("dve", lambda e: e.tensor_copy(out=dord[:, 1:NTILE], in_=dec[:, 1:NTILE]), rd=["dec", "dord"], wr=["dord"])
                    else:
                        P.op("dve", lambda e: e.tensor_copy(out=dord[:, 1:2], in_=dec[:, 0:1]), rd=["dec", "dord"], wr=["dord"])
                        P.op("dve", lambda e: e.tensor_copy(out=dord[:, 2:NTILE], in_=view(dec[:, NTILE - 1:NTILE], [[-1, NTILE - 2]])),
                             rd=["dec", "dord"], wr=["dord"])
                    P.op("dve", lambda e: e.tensor_copy(out=decs[:, :, :], in_=view(dord[:, 0:1], [[0, 128], [1, NTILE]])),
                         rd=["dord"], wr=["decs"])
                    for c0 in range(0, NTILE, 4):
                        cn = min(4, NTILE - c0)
                        for i in range(cn):
                            c = c0 + i
                            P.op("pe", lambda e, c=c, i=i: e.transpose(self.psT[:, i * 64:(i + 1) * 64], kt[:, c * 128:(c + 1) * 128],
                                                                       self.identb[0:64, 0:64]),
                                 rd=["kt", "identb"], wr=["psT"])
                        self.act(kttok[:, c0:c0 + cn, :], self.psT[:, 0:cn * 64].rearrange("p (c d) -> p c d", d=64), AF.Copy,
                                 rd=["psT"], wr=[("kttok", c0)])
                    for g0 in range(0, NTILE, 4):
                        gn = min(4, NTILE - g0)
                        b = (g0 // 4) % 2
                        for i in range(gn):
                            c = g0 + i
                            cs = slice(c * 128, (c + 1) * 128)
                            self.mm(self.ps[b][:, i * 128:(i + 1) * 128], kt[:, cs], qt[:, cs], True, True,
                                    rd=["kt", "qt"], wr=[("ps", b)])
                        self.tt("dve", atall[:, g0:g0 + gn, :], self.ps[b][:, 0:gn * 128].rearrange("p (c l) -> p c l", l=128),
                                view(self.tri[:, z, 0:1], [[0, gn], [1, 128]]), ALU.mult, rd=[("ps", b), "tri"], wr=[("at", g0)])
                    for g0 in range(0, NTILE, 4):
                        gn = min(4, NTILE - g0)
                        b = 2 + (g0 // 4) % 2
                        for i in range(gn):
                            c = g0 + i
                            self.mm(self.ps[b][0:64, i * 128:(i + 1) * 128], kttok[:, c, :], vh[:, c, :], True, True,
                                    rd=[("kttok", (c // 4) * 4), ("vh", c)], wr=[("ps", b)])
                        for i in range(gn):
                            c = g0 + i
                            self.ts("dve", kvd[:, :, pos[c]], self.ps[b][0:64, i * 128:(i + 1) * 128], dec[:, c:c + 1], ALU.mult,
                                    rd=[("ps", b), "dec"], wr=["kvd"])
                    P.op("dve", lambda e: e.tensor_tensor_scan(out=stt_[:, :, :].rearrange("p e c -> p (e c)"),
                                                               data0=decs[:, :, :].rearrange("p e c -> p (e c)"),
                                                               data1=kvd[:, :, :].rearrange("p e c -> p (e c)"), initial=0.0,
                                                               op0=ALU.mult, op1=ALU.add),
                         rd=["decs", "kvd"], wr=["stt"])
                    self.act(Sball[:, 1:NTILE, :], stt_[:, :, 0:NTILE - 1].rearrange("p e c -> p c e"), AF.Copy, rd=["stt"], wr=["Sball"])
                    for g0 in range(0, NTILE, 4):
                        gn = min(4, NTILE - g0)
                        b = 4 + (g0 // 4) % 2
                        for i in range(gn):
                            c = g0 + i
                            cs = slice(c * 128, (c + 1) * 128)
                            first = pos[c] == 0
                            if not first:
                                self.mm(self.ps[b][:, i * 128:(i + 1) * 128], Sball[:, pos[c], :], qt[:, cs], True, False,
                                        rd=["Sball", "qt"], wr=[("ps", b)])
                            self.mm(self.ps[b][:, i * 128:(i + 1) * 128], vh[:, c, :], atall[:, c, :], first, True,
                                    rd=[("vh", c), ("at", (c // 4) * 4)], wr=[("ps", b)])
                        osl = slice(g0 * 128, (g0 + gn) * 128)
                        if z == 0:
                            self.act(oT[:, osl], self.ps[b][:, 0:gn * 128], AF.Copy, rd=[("ps", b)], wr=[("oT", g0)])
                        else:
                            self.tt("dve", oT[:, osl], oT[:, osl], self.ps[b][:, 0:gn * 128], ALU.add,
                                    rd=[("ps", b), ("oT", g0)], wr=[("oT", g0)])
                for bi, (t0, n) in enumerate(TB):
                    if bi == 0 and not upd_ctx:
                        continue
                    okeys = [("oT", g0) for g0 in range(0, NTILE, 4)]
                    sq, sqk = fr.next()
                    self.act(sq[:, 0:n], oT[:, t0:t0 + n], AF.Square, rd=okeys, wr=[sqk])
                    self.mm(self.ps[0][:, 0:n], self.ones32[:, :], sq[:, 0:n], True, True, rd=["ones32", sqk], wr=[("ps", 0)])
                    rs, rsk = fr.next()
                    self.rstd_from(rs[:, 0:n], self.ps[0][:, 0:n], 128, rd=[("ps", 0)], wr=[rsk])
                    self.stt(rs[:, 0:n], oT[:, t0:t0 + n], self.evgl[:, j, h:h + 1], rs[:, 0:n], ALU.mult, ALU.mult,
                             rd=okeys + [rsk, "ev_gla_normT"], wr=[rsk])
                    self.tt("dve", self.bT[:, h, t0:t0 + n], rs[:, 0:n], szb[:, t0:t0 + n], ALU.mult, rd=[rsk, ("szb", bi)],
                            wr=[("bT", bi, h)])
            P.barrier()

    def wout_phase(self, l, wsrc, src, last):
        P, D = self.P, self.D
        upd_ctx = l < 3
        with ExitStack() as st:
            wo = self.sb(st, "wo", [128, 8, D_MODEL], BF16)
            xrot = self.rot(st, "xo", [128, 512], F32, 6)
            for hh in range(2):
                P.dma(wo[:, :, hh * 512:(hh + 1) * 512], wsrc[:, hh * 512:(hh + 1) * 512].rearrange("(k p) n -> p k n", p=128),
                      wr=[("wo", hh)], q="pool")
            for bi, (t0, n) in enumerate(TB):
                if bi == 0 and not upd_ctx:
                    continue
                ci = 1 if bi == 0 else 0
                for fo in range(8):
                    b = (bi * 8 + fo) % 6
                    pb = self.ps[b]
                    for k in range(8):
                        opnd = self.aT[:, k, t0:t0 + n] if k < 4 else self.bT[:, k - 4, t0:t0 + n]
                        self.mm(pb[:, 0:n], wo[:, k, fo * 128:(fo + 1) * 128], opnd, k == 0, k == 7,
                                rd=[("wo", fo // 4), ("actT", bi)], wr=[("ps", b)])
                    xt, xk = xrot.next()
                    P.dma(xt[:, 0:n], src[fo * 128:(fo + 1) * 128, t0:t0 + n], rd=[("xres", bi, fo)], wr=[xk])
                    self.stt(xt[:, 0:n], pb[:, 0:n], self.mod[:, l, 16 + fo, ci:ci + 1], xt[:, 0:n], ALU.mult, ALU.add,
                             rd=[("ps", b), xk, ("mod", l)], wr=[xk])
                    if not last:
                        P.dma(D["xr"][fo * 128:(fo + 1) * 128, t0:t0 + n], xt[:, 0:n], rd=[xk], wr=[("xres", bi, fo), ("xres", bi)])
                    if last and bi > 0:
                        P.dma(D["outT"][fo * 128:(fo + 1) * 128, t0 - NCTX:t0 - NCTX + n], xt[:, 0:n], rd=[xk], wr=[("outT", bi, fo)],
                              is_output=True)
                    if self.debug_full:
                        P.dma(D["dbgT"][fo * 128:(fo + 1) * 128, t0:t0 + n], xt[:, 0:n], rd=[xk], wr=[("dbgT", bi, fo)],
                              is_output=True)
            P.barrier()

    def actT_done(self):
        pass

    def odd_mixer(self, l):
        P, D = self.P, self.D
        j = l // 2
        upd_ctx = l < 3
        win = D["od_w_in"][j]
        rpb = D["od_rpbF"][j]
        OFF = 10
        AT = {0: range(0, 6), 1: range(2, 10), 2: range(6, 14), 3: range(10, 16)}
        with ExitStack() as st:
            wpr = self.rot(st, "wp", [128, 8, 4, 128], BF16, 2)
            qTp = None
            qTz = self.sb(st, "qTz", [128, 2, NT], BF16)
            P.op("pool", lambda e: e.memset(qTz[:], 0.0), wr=["qTz0"])
            kTp = self.sb(st, "kTp", [128, NT], BF16)
            Vp = self.sb(st, "Vp", [128, NTILE, 2, 65], BF16)
            szp = self.sb(st, "szp", [128, NT], F32)
            stg = self.sb(st, "tstg", [128, 15, 64], F32)
            TabFs = [self.sb(st, "TabF%d" % i, [128, 22, 64], F32) for i in range(2)]
            TabIs = [self.sb(st, "TabI%d" % i, [128, 22, 64], F32) for i in range(2)]
            cmask = self.sb(st, "cmask", [128, 64], F32)
            fr = self.rot(st, "f", [128, 512], F32, 8)
            er = self.rot(st, "e", [128, 512], F32, 6)
            ptr = self.rot(st, "pt", [128, 512], BF16, 8)
            P.dma(cmask[:], D["colmask"], wr=["cmask"])
            P.op("pool", lambda e: e.memset(Vp[:, :, :, 64:65], 1.0), wr=[("V1",)])
            for i in range(2):
                P.op("pool", lambda e, i=i: e.memset(TabIs[i][:], 0.0), wr=[("TabI0", i)])
                P.op("pool", lambda e, i=i: e.memset(TabFs[i][:], 0.0), wr=[("TabF", i, 0), ("TabF", i, 1)])
            winv = win.rearrange("(k p) (s c) -> p k s c", p=128, s=4)
            for pr in range(8):
                wp, wpk = wpr.next()
                for s_ in range(4):
                    P.dma(wp[:, :, s_, :], winv[:, :, s_, pr * 128:(pr + 1) * 128], wr=[(wpk, s_)], q="pool")
                for bi, (t0, n) in enumerate(TB):
                  def qkchain(s_, dstT, gcol, bi=bi, t0=t0, n=n):
                        bA = s_ + 4 * (bi % 2)
                        bB = 2 + s_ + 4 * (bi % 2)
                        pb, pk = self.ps[bA], ("ps", bA)
                        if bB == 7:
                            pB, pBk = self.psT[:, :].bitcast(F32), "psT"
                        else:
                            pB, pBk = self.ps[bB], ("ps", bB)
                        self.proj(pb[:, 0:n], lambda k, s_=s_: wp[:, k, s_, :], bi, pk, [(wpk, s_)])
                        yield
                        sq, sqk = fr.next()
                        self.act(sq[:, 0:n], pb[:, 0:n], AF.Square, rd=[pk], wr=[sqk])
                        yield
                        self.mm(pB[:, 0:n], self.bd64[:, :], sq[:, 0:n], True, True, rd=["bd64", sqk], wr=[pBk])
                        yield
                        f_, fk = fr.next()
                        self.rstd_from(f_[:, 0:n], pB[:, 0:n], 64, rd=[pBk], wr=[fk])
                        yield
                        if s_ == 1:
                            self.stt(dstT[:, t0:t0 + n], pb[:, 0:n], self.odg[:, j, gcol:gcol + 1], f_[:, 0:n], ALU.mult, ALU.mult,
                                     rd=[pk, fk, "od_gains"], wr=[("qk", s_, bi)])
                        else:
                            for hq in range(2):
                                r_ = slice(hq * 64, (hq + 1) * 64)
                                self.stt(qTz[r_, hq, t0:t0 + n], pb[r_, 0:n], self.odg[r_, j, gcol:gcol + 1], f_[r_, 0:n],
                                         ALU.mult, ALU.mult, rd=[pk, fk, "od_gains", "qTz0"], wr=[("qk", s_, bi, hq)])
                        yield
                  lockstep([qkchain(0, qTp, 0), qkchain(1, kTp, 1)])
                for bi, (t0, n) in enumerate(TB):
                    pz, pzk = self.ps[4 + bi % 2], ("ps", 4 + bi % 2)
                    self.proj(pz[:, 0:n], lambda k: wp[:, k, 3, :], bi, pzk, [(wpk, 3)])
                    self.act(szp[:, t0:t0 + n], pz[:, 0:n], AF.Silu, rd=[pzk], wr=[("szp", bi, 0), ("szp", bi, 1)])
                for tt_ in range(NTILE):
                    b = 4 + tt_ % 3
                    for k in range(8):
                        self.mm(self.ps[b][:, 0:128], self.hT[:, k, tt_ * 128:(tt_ + 1) * 128], wp[:, k, 2, :], k == 0, k == 7,
                                rd=[("hT", blk_of_tile(tt_)), (wpk, 2)], wr=[("ps", b)])
                    self.act(Vp[:, tt_, :, 0:64], self.ps[b][:, 0:128].rearrange("p (h e) -> p h e", e=64), AF.Copy,
                             rd=[("ps", b)], wr=[("Vp", tt_)])
                tabkeys = {}
                for hh in range(2):
                    h = 2 * pr + hh
                    TabF, TabI = TabFs[hh], TabIs[hh]
                    for half in range(2):
                        src = bass.AP(rpb.tensor, rpb.offset + 64 + h * 465 - 48, [[1, 64], [31, 15], [1, 64]])
                        P.dma(stg[half * 64:(half + 1) * 64, :, :], src, wr=[("tstg", half)])
                    self.act(stg[:, :, :], stg[:, :, :], AF.Exp, rd=[("tstg", 0), ("tstg", 1)], wr=[("tstg", 0), ("tstg", 1)])
                    for half in range(2):
                        p0 = half * 64
                        rv = view(stg[p0:p0 + 64, 0:1, 63:64], [[64, 15], [-1, 64]])
                        cm = view(cmask[p0:p0 + 64, 0:1], [[0, 15], [1, 64]])
                        self.tt("dve", TabF[p0:p0 + 64, 3 + half:18 + half, :], rv, cm, ALU.mult,
                                rd=[("tstg", half), "cmask"], wr=[("TabF", hh, half)])
                        P.op("dve", lambda e, p0=p0, half=half, TabF=TabF, TabI=TabI: e.tensor_copy(
                            out=TabI[p0:p0 + 64, 7 + half:15 + half, :], in_=TabF[p0:p0 + 64, 7 + half:15 + half, :]),
                             rd=[("TabF", hh, half), ("TabI0", hh)], wr=[("TabI", hh, half)])
                    tabkeys[hh] = [("TabF", hh, 0), ("TabF", hh, 1), ("TabI", hh, 0), ("TabI", hh, 1)]
                dstT = self.aT if pr < 4 else self.bT
                for bi, (t0, n) in enumerate(TB):
                    if bi == 0 and not upd_ctx:
                        continue
                    streams = []
                    if bi == 0:
                        keys = [0, 1]
                    else:
                        gq = bi - 1
                        r0 = 8 * gq
                        keys = [0, 1] + [2 + a for a in AT[gq]]
                    for hh in range(2):
                        post = None
                        if bi > 0:
                            def post(i, kt, psb, pskey, gq=gq, r0=r0, n=n, hh=hh):
                                TabF, TabI = TabFs[hh], TabIs[hh]
                                pt, ptk = ptr.next()
                                if kt < 2:
                                    self.act(pt[:, 0:n], psb[:, 0:n], AF.Exp, rd=[pskey], wr=[ptk], scale=C_SCALE)
                                    return pt, ptk
                                a = kt - 2
                                e_, ek = er.next()
                                self.act(e_[:, 0:n], psb[:, 0:n], AF.Exp, rd=[pskey], wr=[ek], scale=C_SCALE)
                                i0 = r0 - 2 * a + OFF
                                if gq == 0 and a <= 3:
                                    parts = [(0, 4, TabF), (4, 8, TabI)]
                                elif gq == 3 and a >= 12:
                                    parts = [(0, 5, TabI), (5, 8, TabF)]
                                else:
                                    parts = [(0, 8, TabI)]
                                for (ra, rb_, tab) in parts:
                                    self.tt("dve", pt[:, ra * 64:rb_ * 64], e_[:, ra * 64:rb_ * 64],
                                            tab[:, i0 + ra:i0 + rb_, :].rearrange("p i c -> p (i c)"), ALU.mult,
                                            rd=[ek] + tabkeys[hh], wr=[ptk])
                                return pt, ptk
                        streams.append(dict(
                            po=self.ps[4 + hh], bo=4 + hh,
                            kfn=lambda kt: kTp[:, kt * 128:(kt + 1) * 128],
                            kkeys=lambda kt: [("qk", 1, blk_of_tile(kt))],
                            q_ap=qTz[:, hh, t0:t0 + n], qkeys=[("qk", 0, bi, hh)],
                            vfn=lambda kt, hh=hh: Vp[:, kt, hh, :], vkeys=lambda kt: [("Vp", kt), ("V1",)], post=post))
                    self.attn_multi(streams, keys, n, C_SCALE, ptr)
                    for hh in range(2):
                        h = 2 * pr + hh
                        dst = dstT[hh * 64:(hh + 1) * 64, pr % 4, t0:t0 + n]
                        self.attn_finish(self.ps[4 + hh], 4 + hh, n, szp[hh * 64:(hh + 1) * 64, t0:t0 + n], ("szp", bi, hh), dst,
                                         [("oT", bi, h)], fr, bbc=hh)
            P.barrier()


def _partner():
    p = np.arange(32)
    half = (p // 8) % 2
    return np.where(half == 0, p + 8, p - 8)


def _rope_tables():
    t = np.arange(NLAT)
    row = (t // 64).astype(np.float32)
    col = (t % 64).astype(np.float32)
    inv = (np.float32(10000.0) ** (-np.arange(0, 16, 2, dtype=np.float32) / np.float32(16))).astype(np.float32)
    tab = np.zeros((2, 128, NLAT), np.float32)
    for d in range(32):
        a, half, f = d // 16, (d // 8) % 2, d % 8
        ang = ((row if a == 0 else col) * inv[f]).astype(np.float32)
        tab[0, 64 + d] = np.cos(ang)
        tab[1, 64 + d] = np.sin(ang) * (-1.0 if half == 0 else 1.0)
    return tab


def _consts():
    m = np.arange(128)
    tri = np.zeros((128, 2, 128), np.float32)
    tri[:, 0, :] = (m[:, None] <= m[None, :])
    tri[:, 1, :] = (m[:, None] >= m[None, :])
    c = np.arange(64)
    cs = np.clip(c - 8, 0, 48)
    kc = np.arange(64)
    cm = ((kc[:, None] >= cs[None, :]) & (kc[:, None] < cs[None, :] + 16)).astype(np.float32)
    colmask = np.concatenate([cm, cm], axis=0)
    bd = np.zeros((128, 128), np.float32)
    bd[:64, :64] = 1
    bd[64:, 64:] = 1
    return dict(rope=_rope_tables(), tri=tri, colmask=np.ascontiguousarray(colmask), bd64=bd)


def prep_inputs(inp):
    f = lambda a: np.ascontiguousarray(np.asarray(a, dtype=np.float32))
    pt = _partner()
    sh = {}
    sh["norm_gT"] = f(inp["norm_g"].reshape(4, 8, 128).transpose(2, 0, 1))
    sh["ada_w"] = f(inp["ada_w"])
    sh["ada_bT"] = f(inp["ada_b"].reshape(4, 24, 128).transpose(2, 0, 1))
    w_in = np.asarray(inp["ev_w_in"])
    sh["ev_w_in"] = f(np.concatenate([w_in, w_in[:, :, 1024 + pt]], axis=2))
    sh["ev_q_normT"] = f(inp["ev_q_norm"].reshape(2, 6, 128).transpose(2, 0, 1))
    w_uq = np.asarray(inp["ev_w_uq"])
    rot_cols = np.concatenate([h * 96 + 64 + pt for h in range(8)])
    sh["ev_w_uq"] = f(np.concatenate([w_uq, w_uq[:, :, rot_cols]], axis=2))
    sh["ev_kv_normT"] = f(inp["ev_kv_norm"].reshape(2, 2, 128).transpose(2, 0, 1))
    sh["ev_w_ukv"] = f(inp["ev_w_ukv"])
    g = np.zeros((128, 2, 4), np.float32)
    qg, kg = np.asarray(inp["ev_q_gain"]), np.asarray(inp["ev_k_gain"])
    for j in range(2):
        g[0:96, j, 0] = qg[j]
        g[64:96, j, 1] = qg[j, 64 + pt]
        g[0:96, j, 2] = kg[j]
        g[64:96, j, 3] = kg[j, 64 + pt]
    sh["ev_gains"] = g
    sh["ev_gate_w"] = f(inp["ev_gate_w"])
    sh["ev_gate_bT"] = f(inp["ev_gate_b"].reshape(2, 2, 4, 64).transpose(3, 0, 1, 2).reshape(64, 2, 8))
    sh["ev_gla_normT"] = f(inp["ev_gla_norm"].reshape(2, 4, 128).transpose(2, 0, 1))
    sh["ev_w_out"] = f(inp["ev_w_out"])
    sh["od_w_in"] = f(inp["od_w_in"])
    og = np.zeros((128, 2, 2), np.float32)
    for j in range(2):
        og[:, j, 0] = np.tile(np.asarray(inp["od_q_gain"])[j], 2)
        og[:, j, 1] = np.tile(np.asarray(inp["od_k_gain"])[j], 2)
    sh["od_gains"] = og
    rpbf = np.asarray(inp["od_rpb"])[:, :, ::-1, :].reshape(2, -1)
    sh["od_rpbF"] = f(np.concatenate([np.zeros((2, 64), np.float32), rpbf, np.zeros((2, 64), np.float32)], axis=1))
    sh["od_w_out"] = f(inp["od_w_out"])
    sh.update(_consts())
    x, ctx, c, c_ctx = (np.asarray(inp[k]) for k in ("x", "ctx", "c", "c_ctx"))
    per = []
    for b in range(x.shape[0]):
        d = dict(sh)
        d["xT"] = f(np.concatenate([ctx[b], x[b]], axis=0).T)
        c2 = np.stack([c[b].reshape(8, 128).T, c_ctx.reshape(8, 128).T], axis=2)
        d["c2"] = f(c2)
        per.append(d)
    return per


_NC_CACHE = {}


def kernel(**inputs):
    per = prep_inputs(inputs)
    key = "full"
    if key not in _NC_CACHE:
        _NC_CACHE[key] = Builder([0, 1, 2, 3]).build()
    nc = _NC_CACHE[key]
    res = run_bass_kernel_spmd(nc, per, core_ids=list(range(len(per))))
    out = np.stack([np.ascontiguousarray(r["outT"].T) for r in res.results], axis=0)
    return out.astype(np.float32)
```

```python
import numpy as np
from contextlib import ExitStack
import concourse.bass as bass
import concourse.mybir as mybir
from concourse.bass_utils import run_bass_kernel_spmd

F32 = mybir.dt.float32
BF16 = mybir.dt.bfloat16
I32 = mybir.dt.int32
AF = mybir.ActivationFunctionType
ALU = mybir.AluOpType
AX = mybir.AxisListType


class _Op:
    __slots__ = ("eng", "fn", "waits", "signal", "sem", "val", "is_dma", "idx")


class Prog:
    ENGS = ("pe", "act", "dve", "pool", "sp")

    def __init__(self, nc, stack, same_sync=True, n_dma_sems=(40, 16, 16)):
        self.nc = nc
        self.same_sync = same_sync
        self.ops = {e: [] for e in self.ENGS}
        self.all_ops = []
        self.state = {}
        self.esem = {e: stack.enter_context(nc.semaphore("s_" + e)) for e in self.ENGS}
        self.dpool = {}
        for q, n in zip(("sp", "act", "pool"), n_dma_sems):
            self.dpool[q] = [[stack.enter_context(nc.semaphore("d_%s%d" % (q, i))), 0, None] for i in range(n)]
        self.dnext = {q: 0 for q in self.dpool}
        self.out_dmas = []

    def _add_wait(self, op, prod):
        if prod is None or prod is op:
            return
        if (not prod.is_dma) and (not op.is_dma) and prod.eng == op.eng:
            if op.eng == "pe" or not self.same_sync:
                return
        if not prod.is_dma:
            prod.signal = True
        if prod not in op.waits:
            op.waits.append(prod)

    def _record(self, op, rd, wr):
        for k in rd:
            st = self.state.get(k)
            if st is None:
                st = self.state[k] = [None, []]
            self._add_wait(op, st[0])
            if k in ("psT", "modps") or (isinstance(k, tuple) and k[0] == "ps"):
                for r in st[1]:
                    if r.eng != op.eng:
                        self._add_wait(op, r)
        for k in wr:
            st = self.state.get(k)
            if st is None:
                st = self.state[k] = [None, []]
            self._add_wait(op, st[0])
            for r in st[1]:
                self._add_wait(op, r)
        for k in rd:
            self.state[k][1].append(op)
        for k in wr:
            self.state[k][0] = op
            self.state[k][1] = []
        op.idx = len(self.all_ops)
        self.all_ops.append(op)
        self.ops[op.eng].append(op)

    def op(self, eng, fn, rd=(), wr=()):
        o = _Op()
        o.eng, o.fn, o.waits, o.signal, o.is_dma = eng, fn, [], False, False
        o.sem, o.val = None, 0
        self._record(o, rd, wr)
        return o

    def dma(self, out, in_, rd=(), wr=(), q="sp", is_output=False, **kw):
        o = _Op()
        o.eng, o.waits, o.signal, o.is_dma = q, [], True, True
        o.fn = lambda e: e.dma_start(out=out, in_=in_, **kw)
        pool = self.dpool[q]
        j = self.dnext[q]
        self.dnext[q] = (j + 1) % len(pool)
        slot = pool[j]
        if slot[2] is not None:
            o.waits.append(slot[2])
        slot[1] += 16
        slot[2] = o
        o.sem, o.val = slot[0], slot[1]
        self._record(o, rd, wr)
        if is_output:
            self.out_dmas.append(o)
        return o

    def barrier(self):
        lasts = []
        for e in self.ENGS:
            for o in reversed(self.ops[e]):
                if o.fn is not None and not o.is_dma:
                    lasts.append(o)
                    break
        dmas = [s[2] for q in self.dpool for s in self.dpool[q] if s[2] is not None]
        for e in ("pe", "act", "dve", "pool", "sp"):
            o = _Op()
            o.eng, o.fn, o.waits, o.signal, o.is_dma = e, None, [], False, False
            o.sem, o.val = None, 0
            for p in lasts + dmas:
                if p.is_dma or p.eng != e:
                    if not p.is_dma:
                        p.signal = True
                    o.waits.append(p)
            o.idx = len(self.all_ops)
            self.all_ops.append(o)
            self.ops[e].append(o)

    def emit(self):
        nc = self.nc
        fin = _Op()
        fin.eng, fin.fn, fin.waits, fin.signal, fin.is_dma = "sp", None, list(self.out_dmas), False, False
        fin.sem, fin.val = None, 0
        self.ops["sp"].append(fin)
        self.all_ops.append(fin)
        cnt = {e: 0 for e in self.ENGS}
        for o in self.all_ops:
            if o.is_dma:
                continue
            if o.signal:
                cnt[o.eng] += 1
                o.sem, o.val = self.esem[o.eng], cnt[o.eng]
        self.sig_counts = cnt

        def run(e, eng):
            waited = {}
            for o in self.ops[e]:
                for p in o.waits:
                    key = id(p.sem)
                    if waited.get(key, 0) >= p.val:
                        continue
                    eng.wait_ge(p.sem, p.val)
                    waited[key] = p.val
                if o.fn is None:
                    continue
                inst = o.fn(eng)
                if o.is_dma:
                    inst.then_inc(o.sem, 16)
                elif o.signal:
                    inst.then_inc(o.sem, 1)

        with nc.Block() as block:
            @block.tensor
            def _(eng):
                run("pe", eng)

            @block.scalar
            def _(eng):
                run("act", eng)

            @block.vector
            def _(eng):
                run("dve", eng)

            @block.gpsimd
            def _(eng):
                run("pool", eng)

            @block.sync
            def _(eng):
                run("sp", eng)


D_MODEL = 1024
NCTX = 256
NLAT = 2048
NT = NCTX + NLAT
NTILE = NT // 128
EPS = 1e-6
TB = [(0, 256)] + [(256 + 512 * i, 512) for i in range(4)]
EV_IN = 3136
A_SCALE = 96 ** -0.5
C_SCALE = 64 ** -0.5


def blk_of_tile(tt):
    return 0 if tt < 2 else 1 + (tt - 2) // 4


def view(ap, dims, off=0):
    return bass.AP(ap.tensor, ap.offset + off, [list(ap.ap[0])] + [list(d) for d in dims])


def lockstep(gens):
    gens = list(gens)
    while gens:
        nxt = []
        for g in gens:
            try:
                next(g)
                nxt.append(g)
            except StopIteration:
                pass
        gens = nxt


class Rot:
    def __init__(self, tiles, name):
        self.tiles, self.name, self.i = tiles, name, 0

    def next(self):
        j = self.i % len(self.tiles)
        self.i += 1
        return self.tiles[j], (self.name, j)


class Builder:
    def __init__(self, layers, debug_full=False, stop_after=None):
        self.stop_after = stop_after
        self.layers = layers
        self.debug_full = debug_full
        self.nc = bass.Bass("TRN2", target_bir_lowering=False)
        self.D = {}

    def din(self, name, shape, dt=F32):
        self.D[name] = self.nc.dram_tensor(name, list(shape), dt, kind="ExternalInput").ap()

    def sb(self, st, name, shape, dt):
        self._uid = getattr(self, "_uid", 0) + 1
        return st.enter_context(self.nc.sbuf_tensor("sb%d_%s" % (self._uid, name), list(shape), dt))

    def rot(self, st, name, shape, dt, n):
        return Rot([self.sb(st, "%s%d" % (name, i), shape, dt) for i in range(n)], name)

    def mm(self, out, lhsT, rhs, start, stop, rd, wr, **kw):
        self.P.op("pe", lambda e: e.matmul(out, lhsT=lhsT, rhs=rhs, start=start, stop=stop, **kw), rd=rd, wr=wr)

    def act(self, out, in_, func, rd, wr, scale=1.0, bias=None):
        if bias is None:
            self.P.op("act", lambda e: e.activation(out=out, in_=in_, func=func, scale=scale), rd=rd, wr=wr)
        else:
            self.P.op("act", lambda e: e.activation(out=out, in_=in_, func=func, scale=scale, bias=bias), rd=rd, wr=wr)

    def tt(self, eng, out, in0, in1, op, rd, wr):
        self.P.op(eng, lambda e: e.tensor_tensor(out=out, in0=in0, in1=in1, op=op), rd=rd, wr=wr)

    def stt(self, out, in0, scalar, in1, op0, op1, rd, wr):
        self.P.op("dve", lambda e: e.scalar_tensor_tensor(out=out, in0=in0, scalar=scalar, in1=in1, op0=op0, op1=op1),
                  rd=rd, wr=wr)

    def ts(self, eng, out, in0, s1, op0, rd, wr, s2=None, op1=None):
        if op1 is None:
            self.P.op(eng, lambda e: e.tensor_scalar(out=out, in0=in0, scalar1=s1, scalar2=None, op0=op0), rd=rd, wr=wr)
        else:
            self.P.op(eng, lambda e: e.tensor_scalar(out=out, in0=in0, scalar1=s1, scalar2=s2, op0=op0, op1=op1),
                      rd=rd, wr=wr)

    def rstd_from(self, dst, src, n_feat, rd, wr):
        self.act(dst, src, AF.Ln, rd=rd, wr=wr, scale=1.0 / n_feat, bias=EPS)
        self.act(dst, dst, AF.Exp, rd=wr, wr=wr, scale=-0.5)

    def proj(self, ps_ap, wfn, bi, pskey, wkeys, **kw):
        t0, n = TB[bi]
        for k in range(8):
            self.mm(ps_ap, wfn(k), self.hT[:, k, t0:t0 + n], k == 0, k == 7,
                    rd=[("hT", bi)] + list(wkeys), wr=[pskey], **kw)

    def build(self):
        nc, D = self.nc, self.D
        self.din("xT", [D_MODEL, NT])
        self.din("c2", [128, 8, 2])
        self.din("norm_gT", [128, 4, 8])
        self.din("ada_w", [4, D_MODEL, 3 * D_MODEL])
        self.din("ada_bT", [128, 4, 24])
        self.din("ev_w_in", [2, D_MODEL, EV_IN + 32])
        self.din("ev_q_normT", [128, 2, 6])
        self.din("ev_w_uq", [2, 768, 768 + 256])
        self.din("ev_kv_normT", [128, 2, 2])
        self.din("ev_w_ukv", [2, 256, 1024])
        self.din("ev_gains", [128, 2, 4])
        self.din("ev_gate_w", [2, 2, 16, 256])
        self.din("ev_gate_bT", [64, 2, 8])
        self.din("ev_gla_normT", [128, 2, 4])
        self.din("ev_w_out", [2, D_MODEL, D_MODEL])
        self.din("od_w_in", [2, D_MODEL, 4096])
        self.din("od_gains", [128, 2, 2])
        self.din("od_rpbF", [2, 16 * 15 * 31 + 128])
        self.din("od_w_out", [2, D_MODEL, D_MODEL])
        self.din("rope", [2, 128, NLAT])
        self.din("tri", [128, 2, 128])
        self.din("colmask", [128, 64])
        self.din("bd64", [128, 128])
        D["outT"] = nc.dram_tensor("outT", [D_MODEL, NLAT], F32, kind="ExternalOutput").ap()
        if self.debug_full:
            D["dbgT"] = nc.dram_tensor("dbgT", [D_MODEL, NT], F32, kind="ExternalOutput").ap()
        D["xr"] = nc.dram_tensor("xr", [D_MODEL, NT], F32, kind="Internal").ap()

        with ExitStack() as st:
            self.P = P = Prog(nc, st, same_sync=True)
            self.ps = [st.enter_context(nc.psum_tensor("ps%d" % i, [128, 512], F32)) for i in range(7)]
            self.psT = st.enter_context(nc.psum_tensor("psT", [128, 1024], BF16))
            self.ones32 = self.sb(st, "ones32", [128, 128], F32)
            self.identb = self.sb(st, "identb", [128, 128], BF16)
            self.tri = self.sb(st, "tri", [128, 2, 128], F32)
            self.bd64 = self.sb(st, "bd64", [128, 128], F32)
            self.c2 = self.sb(st, "c2", [128, 8, 2], F32)
            self.sc2 = self.sb(st, "sc2", [128, 8, 2], BF16)
            self.normg = self.sb(st, "normg", [128, 4, 8], F32)
            self.adab = self.sb(st, "adab", [128, 4, 24], F32)
            self.mod = self.sb(st, "mod", [128, 4, 24, 2], F32)
            self.Amod = self.sb(st, "Amod", [128, 4, 8, 2], F32)
            self.evg = self.sb(st, "evg", [128, 2, 4], F32)
            self.evqn = self.sb(st, "evqn", [128, 2, 6], F32)
            self.evkvn = self.sb(st, "evkvn", [128, 2, 2], F32)
            self.evgb = self.sb(st, "evgb", [64, 2, 8], F32)
            self.evgl = self.sb(st, "evgl", [128, 2, 4], F32)
            self.odg = self.sb(st, "odg", [128, 2, 2], F32)
            self.preamble(st)
            for li, l in enumerate(self.layers if self.stop_after != "pre" else []):
                src = D["xT"] if li == 0 else D["xr"]
                last = li == len(self.layers) - 1
                with ExitStack() as lst:
                    self.hT = self.sb(lst, "hT", [128, 8, NT], BF16)
                    self.norm_phase(l, src, self.layers[li + 1] if li + 1 < len(self.layers) else None)
                    if self.stop_after == "norm":
                        break
                    self.aT = self.sb(lst, "aT", [128, 4, NT], BF16)
                    if l % 2 == 0:
                        self.even_mla(l)
                        if self.stop_after in ("mla_k", "mla"):
                            break
                        self.bT = self.sb(lst, "bT", [128, 4, NT], BF16)
                        self.even_gla(l)
                        if self.stop_after == "gla":
                            break
                        wsrc = D["ev_w_out"][l // 2]
                    else:
                        self.bT = self.sb(lst, "bT", [128, 4, NT], BF16)
                        self.odd_mixer(l)
                        wsrc = D["od_w_out"][l // 2]
                    self.wout_phase(l, wsrc, src, last)
                    P.barrier()
            P.emit()
        return nc

    def preamble(self, st):
        P, D = self.P, self.D
        P.op("pool", lambda e: e.memset(self.ones32[:], 1.0), wr=["ones32"])
        P.op("pool", lambda e: e.memset(self.identb[:], 0.0), wr=["identb"])
        P.op("pool", lambda e: e.affine_select(out=self.identb[:], in_=self.identb[:], pattern=[[-1, 128]],
                                               compare_op=ALU.not_equal, fill=1.0, base=0, channel_multiplier=1),
             rd=["identb"], wr=["identb"])
        for tile, name in ((self.tri, "tri"), (self.bd64, "bd64"), (self.c2, "c2"), (self.normg, "norm_gT"),
                           (self.adab, "ada_bT"), (self.evg, "ev_gains"), (self.evqn, "ev_q_normT"),
                           (self.evkvn, "ev_kv_normT"), (self.evgb, "ev_gate_bT"), (self.evgl, "ev_gla_normT"),
                           (self.odg, "od_gains")):
            P.dma(tile[:], D[name], wr=[name])
        self.act(self.sc2[:], self.c2[:], AF.Silu, rd=["c2"], wr=["sc2"])
        P.op("dve", lambda e: e.tensor_scalar(out=self.evgb[:], in0=self.evgb[:], scalar1=-1.0, scalar2=None, op0=ALU.mult),
             rd=["ev_gate_bT"], wr=["ev_gate_bT"])
        with ExitStack() as pst:
            wrot = self.rot(pst, "adaw", [128, 8, 512], BF16, 2)
            for _ in self.mod_gen(self.layers[0], wrot, self.ps[0], "modps"):
                pass
            P.barrier()

    def mod_gen(self, l, wrot, bank, bkey):
        P, D = self.P, self.D
        modps = bank[:, 0:48].rearrange("p (j t) -> p j t", t=2)
        for g in range(6):
            wt, wk = wrot.next()
            P.dma(wt[:], D["ada_w"][l, :, g * 512:(g + 1) * 512].rearrange("(k p) n -> p k n", p=128),
                  wr=[wk], q="pool")
            for jj in range(4):
                j = g * 4 + jj
                for k in range(8):
                    self.mm(modps[:, j, :], wt[:, k, jj * 128:(jj + 1) * 128], self.sc2[:, k, :], k == 0, k == 7,
                            rd=[wk, "sc2"], wr=[bkey])
            yield
        ab = self.adab[:, l, :]
        self.tt("dve", self.mod[:, l, :, :], modps, view(ab, [[1, 24], [0, 2]]), ALU.add,
                rd=[bkey, "ada_bT"], wr=[("mod", l)])
        ng = self.normg[:, l, :]
        self.stt(self.Amod[:, l, :, :], self.mod[:, l, 8:16, :], 1.0, view(ng, [[1, 8], [0, 2]]), ALU.add, ALU.mult,
                 rd=[("mod", l), "norm_gT"], wr=[("Amod", l)])
        yield

    def norm_phase(self, l, src, l_next=None):
        P = self.P
        with ExitStack() as st:
            mg = None
            if l_next is not None:
                wrot = self.rot(st, "adaw", [128, 8, 512], BF16, 2)
                mg = self.mod_gen(l_next, wrot, self.ps[5], ("ps", 5))
            xrot = self.rot(st, "xt", [128, 8, 512], F32, 2)
            sqrot = self.rot(st, "sq", [128, 8, 512], F32, 1)
            rrot = self.rot(st, "rs", [128, 512], F32, 2)
            trot = self.rot(st, "tn", [128, 512], F32, 3)
            for bi, (t0, n) in enumerate(TB):
                ci = 1 if bi == 0 else 0
                xt, xk = xrot.next()
                sq, sk = sqrot.next()
                rs, rk = rrot.next()
                P.dma(xt[:, :, 0:n], src[:, t0:t0 + n].rearrange("(k p) n -> p k n", p=128), rd=[("xres", bi)], wr=[xk])
                pb = self.ps[bi % 4]
                pk = ("ps", bi % 4)
                for k in range(8):
                    self.act(sq[:, k, 0:n], xt[:, k, 0:n], AF.Square, rd=[xk], wr=[(sk, k)])
                    self.mm(pb[:, 0:n], self.ones32[:, :], sq[:, k, 0:n], k == 0, k == 7, rd=["ones32", (sk, k)], wr=[pk])
                self.act(rs[:, 0:n], pb[:, 0:n], AF.Ln, rd=[pk], wr=[rk], scale=1.0 / D_MODEL, bias=EPS)
                self.act(rs[:, 0:n], rs[:, 0:n], AF.Exp, rd=[rk], wr=[rk], scale=-0.5)
                for k in range(8):
                    tn, tk = trot.next()
                    self.stt(tn[:, 0:n], xt[:, k, 0:n], self.Amod[:, l, k, ci:ci + 1], rs[:, 0:n], ALU.mult, ALU.mult,
                             rd=[xk, rk, ("Amod", l)], wr=[tk])
                    self.act(self.hT[:, k, t0:t0 + n], tn[:, 0:n], AF.Identity, rd=[tk, ("mod", l)], wr=[("hT", bi)],
                             bias=self.mod[:, l, k, ci:ci + 1])
                if mg is not None:
                    next(mg, None)
            if mg is not None:
                for _ in mg:
                    pass
            P.barrier()

    def even_mla(self, l):
        P, D = self.P, self.D
        j = l // 2
        upd_ctx = l < 3
        win = D["ev_w_in"][j]
        gq = lambda a, b: self.evg[a:b, j, 0:1]
        gqr = lambda a, b: self.evg[a:b, j, 1:2]
        gk = lambda a, b: self.evg[a:b, j, 2:3]
        gkr = lambda a, b: self.evg[a:b, j, 3:4]
        with ExitStack() as st:
            kT = self.sb(st, "kT", [128, 8, NT], BF16)
            Vt = self.sb(st, "Vt", [128, NTILE, 8, 65], BF16)
            wq = self.sb(st, "wq", [128, 8, 768], BF16)
            wuq = self.sb(st, "wuq", [128, 6, 1024], BF16)
            wza = self.sb(st, "wza", [128, 8, 512], BF16)
            fr = self.rot(st, "f", [128, 512], F32, 8)
            rp = self.rot(st, "ropeb", [128, 2, 512], F32, 2)
            P.op("pool", lambda e: e.memset(Vt[:, :, :, 64:65], 1.0), wr=[("V1",)])
            P.dma(wq[:], win[:, 0:768].rearrange("(k p) n -> p k n", p=128), wr=["wq"], q="pool")
            P.dma(wza[:], win[:, 1056:1568].rearrange("(k p) n -> p k n", p=128), wr=["wza"], q="pool")

            def load_rope(bi):
                t0, n = TB[bi]
                rb, rbk = rp.next()
                P.dma(rb[64:96, :, :], D["rope"][:, 64:96, t0 - NCTX:t0 - NCTX + n].rearrange("c p n -> p c n"), wr=[rbk])
                return rb, rbk

            with ExitStack() as ks:
                wkv = self.sb(ks, "wkv", [128, 8, 320], BF16)
                wukv = self.sb(ks, "wukv", [128, 2, 1024], BF16)
                stg = self.rot(ks, "stg", [128, 1024], F32, 2)
                kvlr = self.rot(ks, "kvl", [128, 2, 512], BF16, 2)
                sqkr = self.rot(ks, "sqk", [128, 2, 512], F32, 1)
                kst = self.rot(ks, "kst", [128, 512], F32, 6)
                rsv = self.rot(ks, "rsv", [128, 4], F32, 2)
                sqrz = self.sb(ks, "sqrz", [128, 512], F32)
                P.op("pool", lambda e: e.memset(sqrz[:], 0.0), wr=["sqrz"])
                P.dma(wkv[:, :, 0:288], win[:, 768:1056].rearrange("(k p) n -> p k n", p=128), wr=["wkv_a"], q="pool")
                P.dma(wkv[:, :, 288:320], win[:, 3136:3168].rearrange("(k p) n -> p k n", p=128), wr=["wkv_b"], q="pool")
                for c in range(2):
                    sg, sgk = stg.next()
                    P.dma(sg[:, :], D["ev_w_ukv"][j, c * 128:(c + 1) * 128, :], wr=[sgk])
                    self.ts("dve", wukv[:, c, :], sg[:, :], self.evkvn[:, j, c:c + 1], ALU.mult,
                            rd=[sgk, "ev_kv_normT"], wr=[("wukv", c)])
                for c in range(6):
                    sg, sgk = stg.next()
                    P.dma(sg[:, :], D["ev_w_uq"][j, c * 128:(c + 1) * 128, :], wr=[sgk])
                    self.ts("dve", wuq[:, c, :], sg[:, :], self.evqn[:, j, c:c + 1], ALU.mult,
                            rd=[sgk, "ev_q_normT"], wr=[("wuq", c)])
                wukv_keys = [("wukv", 0), ("wukv", 1)]
                for bi, (t0, n) in enumerate(TB):
                    lat = bi > 0
                    nt = n // 128
                    kvl, kvk = kvlr.next()
                    sqk, sqkk = sqkr.next()
                    for c in range(2):
                        pb = self.ps[c]
                        self.proj(pb[:, 0:n], lambda k, c=c: wkv[:, k, c * 128:(c + 1) * 128], bi, ("ps", c), ["wkv_a"])
                        self.act(kvl[:, c, 0:n], pb[:, 0:n], AF.Copy, rd=[("ps", c)], wr=[(kvk, c)])
                        self.act(sqk[:, c, 0:n], pb[:, 0:n], AF.Square, rd=[("ps", c)], wr=[(sqkk, c)])
                    for c in range(2):
                        self.mm(self.ps[2][:, 0:n], self.ones32[:, :], sqk[:, c, 0:n], c == 0, c == 1,
                                rd=["ones32", (sqkk, c)], wr=[("ps", 2)])
                    ak_, akk = kst.next()
                    svk, svkk = kst.next()
                    ck, ckk = kst.next()
                    self.act(ak_[:, 0:n], self.ps[2][:, 0:n], AF.Ln, rd=[("ps", 2)], wr=[akk], scale=1.0 / 256, bias=EPS)
                    self.act(svk[:, 0:n], ak_[:, 0:n], AF.Exp, rd=[akk], wr=[svkk], scale=0.5)
                    self.ts("dve", ck[:, 0:n], self.ps[2][:, 0:n], 1.0 / 256, ALU.mult, rd=[("ps", 2)], wr=[ckk], s2=EPS, op1=ALU.add)
                    rv, rvk = rsv.next()
                    for ti in range(nt):
                        for c in range(2):
                            self.mm(self.ps[3][:, ti:ti + 1], sqk[:, c, ti * 128:(ti + 1) * 128], self.ones32[:, 0:1],
                                    c == 0, c == 1, rd=["ones32", (sqkk, c)], wr=[("ps", 3)])
                    self.act(rv[:, 0:nt], self.ps[3][:, 0:nt], AF.Ln, rd=[("ps", 3)], wr=[rvk], scale=1.0 / 256, bias=EPS)
                    self.act(rv[:, 0:nt], rv[:, 0:nt], AF.Exp, rd=[rvk], wr=[rvk], scale=-0.5)
                    self.proj(self.ps[4][64:96, 0:n], lambda k: wkv[:, k, 256:288], bi, ("ps", 4), ["wkv_a"])
                    if lat:
                        self.proj(self.ps[5][64:96, 0:n], lambda k: wkv[:, k, 288:320], bi, ("ps", 5), ["wkv_b"])
                        rb, rbk = load_rope(bi)
                    sqr, sqrk = sqrz, "sqrz"
                    self.act(sqr[64:96, 0:n], self.ps[4][64:96, 0:n], AF.Square, rd=[("ps", 4)], wr=[sqrk])
                    self.mm(self.ps[6][:, 0:n], self.ones32[:, :], sqr[:, 0:n], True, True,
                            rd=["ones32", sqrk], wr=[("ps", 6)])
                    ssr, ssrk = kst.next()
                    self.ts("dve", ssr[:, 0:n], self.ps[6][:, 0:n], 1.0 / 96, ALU.mult, rd=[("ps", 6)], wr=[ssrk], s2=EPS, op1=ALU.add)
                    self.tt("dve", ck[:, 0:n], ck[:, 0:n], ssr[:, 0:n], ALU.mult, rd=[ckk, ssrk], wr=[ckk])
                    R, Rk = kst.next()
                    if lat:
                        t1, t1k = fr.next()
                        t2, t2k = fr.next()
                        self.stt(t1[64:96, 0:n], self.ps[4][64:96, 0:n], gk(64, 96), rb[64:96, 0, 0:n], ALU.mult, ALU.mult,
                                 rd=[("ps", 4), rbk, "ev_gains"], wr=[t1k])
                        self.stt(t2[64:96, 0:n], self.ps[5][64:96, 0:n], gkr(64, 96), rb[64:96, 1, 0:n], ALU.mult, ALU.mult,
                                 rd=[("ps", 5), rbk, "ev_gains"], wr=[t2k])
                        self.tt("dve", R[64:96, 0:n], t1[64:96, 0:n], t2[64:96, 0:n], ALU.add, rd=[t1k, t2k], wr=[Rk])
                    else:
                        self.act(R[64:96, 0:n], self.ps[4][64:96, 0:n], AF.Copy, rd=[("ps", 4), "ev_gains"], wr=[Rk],
                                 scale=gk(64, 96))
                    self.tt("dve", R[64:96, 0:n], R[64:96, 0:n], svk[64:96, 0:n], ALU.mult, rd=[Rk, svkk], wr=[Rk])
                    for ti in range(nt):
                        tt_ = t0 // 128 + ti
                        pv = self.ps[5 + (ti % 2)] if not lat else self.ps[3 + 3 * (ti % 2)]
                        pvk = ("ps", 5 + (ti % 2)) if not lat else ("ps", 3 + 3 * (ti % 2))
                        for c in range(2):
                            vcols = wukv[:, c, :].rearrange("p (h e) -> p h e", e=128)[:, :, 64:128]
                            self.mm(pv[:, :], kvl[:, c, ti * 128:(ti + 1) * 128], vcols, c == 0, c == 1,
                                    rd=[(kvk, c)] + wukv_keys, wr=[pvk])
                        self.act(Vt[:, tt_, :, 0:64], pv[:, :].rearrange("p (h e) -> p h e", e=64), AF.Copy,
                                 rd=[pvk, rvk], wr=[("V", tt_)], scale=rv[:, ti:ti + 1])
                    def khead(h):
                        pn = self.ps[h % 2]
                        pnk = ("ps", h % 2)
                        bs = 2 if h % 2 == 0 else 5
                        for c in range(2):
                            self.mm(pn[0:64, 0:n], wukv[:, c, h * 128:h * 128 + 64], kvl[:, c, 0:n], c == 0, c == 1,
                                    rd=[(kvk, c)] + wukv_keys, wr=[pnk])
                        yield
                        sqn, sqnk = fr.next()
                        self.act(sqn[0:64, 0:n], pn[0:64, 0:n], AF.Square, rd=[pnk], wr=[sqnk])
                        yield
                        self.mm(self.ps[bs][:, 0:n], self.ones32[0:64, :], sqn[0:64, 0:n], True, True,
                                rd=["ones32", sqnk], wr=[("ps", bs)])
                        yield
                        tb_, tbk = fr.next()
                        self.stt(tb_[:, 0:n], self.ps[bs][:, 0:n], 1.0 / 96, ck[:, 0:n], ALU.mult, ALU.add, rd=[("ps", bs), ckk], wr=[tbk])
                        yield
                        self.act(tb_[:, 0:n], tb_[:, 0:n], AF.Ln, rd=[tbk], wr=[tbk])
                        self.act(tb_[:, 0:n], tb_[:, 0:n], AF.Exp, rd=[tbk], wr=[tbk], scale=-0.5)
                        yield
                        self.stt(kT[0:64, h, t0:t0 + n], pn[0:64, 0:n], gk(0, 64), tb_[0:64, 0:n], ALU.mult, ALU.mult,
                                 rd=[pnk, tbk, "ev_gains"], wr=[("kT", bi, h, 0)])
                        self.tt("dve", kT[64:96, h, t0:t0 + n], R[64:96, 0:n], tb_[64:96, 0:n], ALU.mult,
                                rd=[Rk, tbk], wr=[("kT", bi, h, 1)])
                        yield

                    for h0 in range(0, 8, 2):
                        lockstep([khead(h0), khead(h0 + 1)])
                P.barrier()

            if self.stop_after == "mla_k":
                return
            with ExitStack() as qs:
                qlb = self.sb(qs, "qlb", [128, 6, 512], BF16)
                qT = self.sb(qs, "qT", [128, 8, 512], BF16)
                rqt = self.sb(qs, "rqt", [128, 512], F32)
                eqt = self.sb(qs, "eqt", [128, 512], F32)
                ptr = self.rot(qs, "pt", [128, 512], BF16, 5)
                sz8 = self.sb(qs, "sz8", [64, 8, 512], F32)
                for bi, (t0, n) in enumerate(TB):
                    lat = bi > 0
                    if not lat and not upd_ctx:
                        continue
                    if lat:
                        rb, rbk = load_rope(bi)
                    for jq in range(6):
                        pb = self.ps[jq % 2]
                        pk = ("ps", jq % 2)
                        self.proj(pb[:, 0:n], lambda k, jq=jq: wq[:, k, jq * 128:(jq + 1) * 128], bi, pk, ["wq"])
                        self.act(qlb[:, jq, 0:n], pb[:, 0:n], AF.Copy, rd=[pk], wr=[("qlb", jq)])
                        sq, sqk_ = fr.next()
                        self.act(sq[:, 0:n], pb[:, 0:n], AF.Square, rd=[pk], wr=[sqk_])
                        self.mm(self.ps[2][:, 0:n], self.ones32[:, :], sq[:, 0:n], jq == 0, jq == 5,
                                rd=["ones32", sqk_], wr=[("ps", 2)])
                    self.ts("dve", eqt[:, 0:n], self.ps[2][:, 0:n], EPS / 768, ALU.mult, rd=[("ps", 2)], wr=["eqt"],
                            s2=EPS * EPS, op1=ALU.add)
                    qlb_keys = [("qlb", jq) for jq in range(6)]
                    wuq_keys = [("wuq", c) for c in range(6)]
                    def qhead(h):
                        b3, b4, b5 = (3, 4, 5) if h % 2 == 0 else (6, 0, 1)
                        p3, p4, p5 = self.ps[b3], self.ps[b4], self.ps[b5]
                        for jq in range(6):
                            self.mm(p3[0:96, 0:n], wuq[:, jq, h * 96:(h + 1) * 96], qlb[:, jq, 0:n], jq == 0, jq == 5,
                                    rd=qlb_keys + wuq_keys, wr=[("ps", b3)])
                        if lat:
                            for jq in range(6):
                                self.mm(p4[64:96, 0:n], wuq[:, jq, 768 + h * 32:768 + (h + 1) * 32], qlb[:, jq, 0:n],
                                        jq == 0, jq == 5, rd=qlb_keys + wuq_keys, wr=[("ps", b4)])
                        yield
                        sqh, sqhk = fr.next()
                        self.act(sqh[0:96, 0:n], p3[0:96, 0:n], AF.Square, rd=[("ps", b3)], wr=[sqhk])
                        yield
                        self.mm(p5[:, 0:n], self.ones32[0:96, :], sqh[0:96, 0:n], True, True, rd=["ones32", sqhk], wr=[("ps", b5)])
                        yield
                        tb_, tbk = fr.next()
                        self.stt(tb_[:, 0:n], p5[:, 0:n], 1.0 / 96, eqt[:, 0:n], ALU.mult, ALU.add, rd=[("ps", b5), "eqt"], wr=[tbk])
                        yield
                        self.act(tb_[:, 0:n], tb_[:, 0:n], AF.Ln, rd=[tbk], wr=[tbk])
                        self.act(tb_[:, 0:n], tb_[:, 0:n], AF.Exp, rd=[tbk], wr=[tbk], scale=-0.5)
                        yield
                        if lat:
                            self.stt(qT[0:64, h, 0:n], p3[0:64, 0:n], gq(0, 64), tb_[0:64, 0:n], ALU.mult, ALU.mult,
                                     rd=[("ps", b3), tbk, "ev_gains"], wr=[("qT", h, 0)])
                            t1, t1k = fr.next()
                            t2, t2k = fr.next()
                            self.stt(t1[64:96, 0:n], p3[64:96, 0:n], gq(64, 96), rb[64:96, 0, 0:n], ALU.mult, ALU.mult,
                                     rd=[("ps", b3), rbk, "ev_gains"], wr=[t1k])
                            self.stt(t2[64:96, 0:n], p4[64:96, 0:n], gqr(64, 96), rb[64:96, 1, 0:n], ALU.mult, ALU.mult,
                                     rd=[("ps", b4), rbk, "ev_gains"], wr=[t2k])
                            self.tt("dve", t1[64:96, 0:n], t1[64:96, 0:n], t2[64:96, 0:n], ALU.add, rd=[t1k, t2k], wr=[t1k])
                            self.tt("dve", qT[64:96, h, 0:n], t1[64:96, 0:n], tb_[64:96, 0:n], ALU.mult,
                                    rd=[t1k, tbk], wr=[("qT", h, 1)])
                        else:
                            self.stt(qT[0:96, h, 0:n], p3[0:96, 0:n], gq(0, 96), tb_[0:96, 0:n], ALU.mult, ALU.mult,
                                     rd=[("ps", b3), tbk, "ev_gains"], wr=[("qT", h, 0), ("qT", h, 1)])
                        yield

                    for h0 in range(0, 8, 2):
                        lockstep([qhead(h0), qhead(h0 + 1)])
                    for h in range(8):
                        bz = 6 if h % 2 == 0 else 3
                        self.proj(self.ps[bz][0:64, 0:n], lambda k, h=h: wza[:, k, h * 64:(h + 1) * 64], bi, ("ps", bz), ["wza"])
                        self.act(sz8[:, h, 0:n], self.ps[bz][0:64, 0:n], AF.Silu, rd=[("ps", bz)], wr=[("sz8", h)])
                    keys = [0, 1] if not lat else list(range(NTILE))
                    for h0 in range(0, 8, 2):
                        streams = []
                        for h in (h0, h0 + 1):
                            streams.append(dict(
                                po=self.ps[4 + (h % 2)], bo=4 + (h % 2),
                                kfn=lambda kt, h=h: kT[0:96, h, kt * 128:(kt + 1) * 128],
                                kkeys=lambda kt, h=h: [("kT", blk_of_tile(kt), h, 0), ("kT", blk_of_tile(kt), h, 1)],
                                q_ap=qT[0:96, h, 0:n], qkeys=[("qT", h, 0), ("qT", h, 1)],
                                vfn=lambda kt, h=h: Vt[:, kt, h, :], vkeys=lambda kt: [("V", kt), ("V1",)], post=None))
                        self.attn_multi(streams, keys, n, A_SCALE, ptr)
                        for h in (h0, h0 + 1):
                            dst = self.aT[(h % 2) * 64:(h % 2) * 64 + 64, h // 2, t0:t0 + n]
                            self.attn_finish(self.ps[4 + (h % 2)], 4 + (h % 2), n, sz8[:, h, 0:n], ("sz8", h), dst,
                                             [("aT", bi, h)], fr, bbc=(0 if h % 2 == 0 else 1))
                P.barrier()

    def attn_core(self, po, bo, keys, n, kfn, kkeys, q_ap, qkeys, vfn, vkeys, scale, ptr, post):
        self.attn_multi([dict(po=po, bo=bo, kfn=kfn, kkeys=kkeys, q_ap=q_ap, qkeys=qkeys, vfn=vfn, vkeys=vkeys, post=post)],
                        keys, n, scale, ptr)

    def attn_multi(self, streams, keys, n, scale, ptr):
        nk = len(keys)
        ns = len(streams)
        SB = [0, 1, 2, 6] if ns == 1 else [0, 1, 2, 6, 3, 7]
        LA = 3 if ns == 1 else 2
        seq = [(i, si) for i in range(nk) for si in range(ns)]
        nseq = len(seq)

        def bank(j):
            b = SB[j % len(SB)]
            if b == 7:
                return self.psT[:, :].bitcast(F32), "psT"
            return self.ps[b], ("ps", b)

        def s_mm(j):
            i, si = seq[j]
            st = streams[si]
            kt = keys[i]
            pb, pk = bank(j)
            self.mm(pb[:, 0:n], st["kfn"](kt), st["q_ap"], True, True, rd=list(st["kkeys"](kt)) + list(st["qkeys"]), wr=[pk])

        ahead = LA * ns
        for j in range(min(ahead, nseq)):
            s_mm(j)
        for j in range(nseq):
            if j + ahead < nseq:
                s_mm(j + ahead)
            i, si = seq[j]
            st = streams[si]
            kt = keys[i]
            pb, pk = bank(j)
            if st["post"] is None:
                pt, ptk = ptr.next()
                self.act(pt[:, 0:n], pb[:, 0:n], AF.Exp, rd=[pk], wr=[ptk], scale=scale)
            else:
                pt, ptk = st["post"](i, kt, pb, pk)
            self.mm(st["po"][0:65, 0:n], st["vfn"](kt), pt[:, 0:n], i == 0, i == nk - 1,
                    rd=list(st["vkeys"](kt)) + [ptk], wr=[("ps", st["bo"])])

    def attn_finish(self, po, bo, n, gate_ap, gate_key, dst, dst_keys, fr, bbc=3):
        rd_, rdk = fr.next()
        self.act(rd_[64:65, 0:n], po[64:65, 0:n], AF.Ln, rd=[("ps", bo)], wr=[rdk])
        self.act(rd_[64:65, 0:n], rd_[64:65, 0:n], AF.Exp, rd=[rdk], wr=[rdk], scale=-1.0)
        pbc = self.ps[bbc]
        self.mm(pbc[0:64, 0:n], self.ones32[64:65, 0:64], rd_[64:65, 0:n], True, True, rd=["ones32", rdk], wr=[("ps", bbc)])
        tm, tmk = fr.next()
        self.tt("dve", tm[0:64, 0:n], po[0:64, 0:n], gate_ap, ALU.mult, rd=[("ps", bo), gate_key], wr=[tmk])
        self.tt("dve", dst, tm[0:64, 0:n], pbc[0:64, 0:n], ALU.mult, rd=[tmk, ("ps", bbc)], wr=dst_keys)

    def even_gla(self, l):
        P, D = self.P, self.D
        j = l // 2
        upd_ctx = l < 3
        win = D["ev_w_in"][j]
        with ExitStack() as st:
            wg = self.sb(st, "wg", [128, 8, 32], BF16)
            gw = self.sb(st, "gw", [16, 2, 256], BF16)
            rT = self.sb(st, "rT", [16, 2, NT], BF16)
            rmask = self.sb(st, "rmask", [64, NT], F32)
            whr = self.rot(st, "wh", [128, 8, 384], BF16, 2)
            qTh = self.sb(st, "qTh", [64, NT], BF16)
            kTh = self.sb(st, "kTh", [64, NT], BF16)
            vh = self.sb(st, "vh", [128, NTILE, 128], BF16)
            oT = self.sb(st, "oT", [128, NT], F32)
            spT = self.sb(st, "spT", [64, NT], F32)
            csT = self.sb(st, "csT", [64, NT], F32)
            qt = self.sb(st, "qt", [64, NT], BF16)
            kt = self.sb(st, "kt", [64, NT], BF16)
            kttok = self.sb(st, "kttok", [128, NTILE, 64], BF16)
            dec = self.sb(st, "dec", [64, NTILE], F32)
            dord = self.sb(st, "dord", [64, NTILE], F32)
            decs = self.sb(st, "decs", [64, 128, NTILE], F32)
            kvd = self.sb(st, "kvd", [64, 128, NTILE], F32)
            stt_ = self.sb(st, "stt", [64, 128, NTILE], F32)
            Sball = self.sb(st, "Sball", [64, NTILE, 128], BF16)
            atall = self.sb(st, "atall", [128, NTILE, 128], BF16)
            szb = self.sb(st, "szb", [128, NT], BF16)
            fr = self.rot(st, "g", [128, 512], F32, 3)
            P.dma(wg[:], win[:, 2592:2624].rearrange("(k p) n -> p k n", p=128), wr=["wg"], q="pool")
            P.dma(gw[:], D["ev_gate_w"][j].rearrange("z r k -> r z k"), wr=["gw"], q="pool")
            P.op("dve", lambda e: e.memset(rmask[:], 1.0), wr=["rmask"])
            rmv = rmask[:].rearrange("p (c l) -> p c l", l=128)
            P.op("dve", lambda e: e.memset(rmv[:, :, 0:1], 0.0), rd=["rmask"], wr=["rmask"])
            for bi, (t0, n) in enumerate(TB):
                for z in range(2):
                    self.proj(self.ps[z][0:16, 0:n], lambda k, z=z: wg[:, k, z * 16:(z + 1) * 16], bi, ("ps", z), ["wg"])
                    self.act(rT[0:16, z, t0:t0 + n], self.ps[z][0:16, 0:n], AF.Copy, rd=[("ps", z)], wr=[("rT", bi, z)])
            rT_keys = lambda z: [("rT", bi, z) for bi in range(len(TB))]
            for h in range(4):
                wh, whk = whr.next()
                for (c0, w, d0) in ((1568 + 64 * h, 64, 0), (1824 + 64 * h, 64, 64), (2080 + 128 * h, 128, 128),
                                    (2624 + 128 * h, 128, 256)):
                    P.dma(wh[:, :, d0:d0 + w], win[:, c0:c0 + w].rearrange("(k p) n -> p k n", p=128), wr=[(whk, d0)], q="pool")
                for bi, (t0, n) in enumerate(TB):
                    bq, bk = 2 * (bi % 3), 2 * (bi % 3) + 1
                    self.proj(self.ps[bq][0:64, 0:n], lambda k: wh[:, k, 0:64], bi, ("ps", bq), [(whk, 0)])
                    self.act(qTh[:, t0:t0 + n], self.ps[bq][0:64, 0:n], AF.Copy, rd=[("ps", bq)], wr=[("qTh", bi)], scale=0.125)
                    self.proj(self.ps[bk][0:64, 0:n], lambda k: wh[:, k, 64:128], bi, ("ps", bk), [(whk, 64)])
                    self.act(kTh[:, t0:t0 + n], self.ps[bk][0:64, 0:n], AF.Copy, rd=[("ps", bk)], wr=[("kTh", bi)])
                for tt_ in range(NTILE):
                    b = tt_ % 4
                    for k in range(8):
                        self.mm(self.ps[b][:, 0:128], self.hT[:, k, tt_ * 128:(tt_ + 1) * 128], wh[:, k, 128:256], k == 0, k == 7,
                                rd=[("hT", blk_of_tile(tt_)), (whk, 128)], wr=[("ps", b)])
                    self.act(vh[:, tt_, :], self.ps[b][:, 0:128], AF.Copy, rd=[("ps", b)], wr=[("vh", tt_)])
                for bi, (t0, n) in enumerate(TB):
                    if bi == 0 and not upd_ctx:
                        continue
                    bz = bi % 2
                    self.proj(self.ps[bz][:, 0:n], lambda k: wh[:, k, 256:384], bi, ("ps", bz), [(whk, 256)])
                    self.act(szb[:, t0:t0 + n], self.ps[bz][:, 0:n], AF.Silu, rd=[("ps", bz)], wr=[("szb", bi)])
                allq = [("qTh", bi) for bi in range(len(TB))]
                allk = [("kTh", bi) for bi in range(len(TB))]
                for z in range(2):
                    eT = spT
                    for bi, (t0, n) in enumerate(TB):
                        b = 4 + bi % 2
                        self.mm(self.ps[b][0:64, 0:n], gw[0:16, z, h * 64:(h + 1) * 64], rT[0:16, z, t0:t0 + n], True, True,
                                rd=["gw", ("rT", bi, z)], wr=[("ps", b)])
                        self.act(spT[:, t0:t0 + n], self.ps[b][0:64, 0:n], AF.Exp, rd=[("ps", b), "ev_gate_bT"], wr=["spT"],
                                 scale=-1.0, bias=self.evgb[:, j, z * 4 + h:z * 4 + h + 1])
                        self.act(spT[:, t0:t0 + n], spT[:, t0:t0 + n], AF.Ln, rd=["spT"], wr=["spT"], bias=1.0)
                    if z == 0:
                        P.op("dve", lambda e: e.tensor_tensor_scan(out=csT[:, :], data0=rmask[:, :], data1=spT[:, :], initial=0.0,
                                                                   op0=ALU.mult, op1=ALU.add),
                             rd=["rmask", "spT"], wr=["csT"])
                    else:
                        rev = lambda t: view(t[:, NT - 1:NT], [[-1, NT]])
                        P.op("dve", lambda e: e.tensor_tensor_scan(out=rev(csT), data0=rmask[:, :], data1=rev(spT), initial=0.0,
                                                                   op0=ALU.mult, op1=ALU.add),
                             rd=["rmask", "spT"], wr=["csT"])
                    self.act(eT[:, :], csT[:, :], AF.Exp, rd=["csT"], wr=["spT"], scale=-1.0 / 16)
                    self.tt("dve", qt[:, :], qTh[:, :], eT[:, :], ALU.mult, rd=allq + ["spT"], wr=["qt"])
                    self.act(eT[:, :], csT[:, :], AF.Exp, rd=["csT"], wr=["spT"], scale=1.0 / 16)
                    self.tt("dve", kt[:, :], kTh[:, :], eT[:, :], ALU.mult, rd=allk + ["spT"], wr=["kt"])
                    order = list(range(NTILE)) if z == 0 else [1, 0] + list(range(NTILE - 1, 1, -1))
                    pos = {c: i for i, c in enumerate(order)}
                    endcol = 127 if z == 0 else 0
                    self.act(dec[:, :], view(csT[:, endcol:endcol + 1], [[128, NTILE]]), AF.Exp, rd=["csT"], wr=["dec"],
                             scale=-1.0 / 16)
                    P.op("dve", lambda e: e.memset(dord[:, 0:1], 0.0), rd=["dec"], wr=["dord"])
                    if z == 0:
                        P.op("dve", lambda e: e.tensor_copy(out=dord[:, 1:NTILE], in_=dec[:, 1:NTILE]), rd=["dec", "dord"], wr=["dord"])
                    else:
                        P.op("dve", lambda e: e.tensor_copy(out=dord[:, 1:2], in_=dec[:, 0:1]), rd=["dec", "dord"], wr=["dord"])
                        P.op("dve", lambda e: e.tensor_copy(out=dord[:, 2:NTILE], in_=view(dec[:, NTILE - 1:NTILE], [[-1, NTILE - 2]])),
                             rd=["dec", "dord"], wr=["dord"])
                    P.op("dve", lambda e: e.tensor_copy(out=decs[:, :, :], in_=view(dord[:, 0:1], [[0, 128], [1, NTILE]])),
                         rd=["dord"], wr=["decs"])
                    for c0 in range(0, NTILE, 4):
                        cn = min(4, NTILE - c0)
                        for i in range(cn):
                            c = c0 + i
                            P.op("pe", lambda e, c=c, i=i: e.transpose(self.psT[:, i * 64:(i + 1) * 64], kt[:, c * 128:(c + 1) * 128],
                                                                       self.identb[0:64, 0:64]),
                                 rd=["kt", "identb"], wr=["psT"])
                        self.act(kttok[:, c0:c0 + cn, :], self.psT[:, 0:cn * 64].rearrange("p (c d) -> p c d", d=64), AF.Copy,
                                 rd=["psT"], wr=[("kttok", c0)])
                    for g0 in range(0, NTILE, 4):
                        gn = min(4, NTILE - g0)
                        b = (g0 // 4) % 2
                        for i in range(gn):
                            c = g0 + i
                            cs = slice(c * 128, (c + 1) * 128)
                            self.mm(self.ps[b][:, i * 128:(i + 1) * 128], kt[:, cs], qt[:, cs], True, True,
                                    rd=["kt", "qt"], wr=[("ps", b)])
                        self.tt("dve", atall[:, g0:g0 + gn, :], self.ps[b][:, 0:gn * 128].rearrange("p (c l) -> p c l", l=128),
                                view(self.tri[:, z, 0:1], [[0, gn], [1, 128]]), ALU.mult, rd=[("ps", b), "tri"], wr=[("at", g0)])
                    for g0 in range(0, NTILE, 4):
                        gn = min(4, NTILE - g0)
                        b = 2 + (g0 // 4) % 2
                        for i in range(gn):
                            c = g0 + i
                            self.mm(self.ps[b][0:64, i * 128:(i + 1) * 128], kttok[:, c, :], vh[:, c, :], True, True,
                                    rd=[("kttok", (c // 4) * 4), ("vh", c)], wr=[("ps", b)])
                        runs = []
                        i = 0
                        while i < gn:
                            k_ = i + 1
                            if k_ < gn:
                                step = pos[g0 + k_] - pos[g0 + i]
                                while k_ < gn and abs(step) == 1 and pos[g0 + k_] - pos[g0 + k_ - 1] == step:
                                    k_ += 1
                            else:
                                step = 1
                            runs.append((i, k_, step if k_ - i > 1 else 1))
                            i = k_
                        for (ia, ib, step) in runs:
                            ca = g0 + ia
                            ln = ib - ia
                            outv = view(kvd[:, 0:1, pos[ca]:pos[ca] + 1], [[step, ln], [NTILE, 128]])
                            self.tt("dve", outv, self.ps[b][0:64, ia * 128:ib * 128].rearrange("p (c e) -> p c e", e=128),
                                    view(dec[:, ca:ca + 1], [[1, ln], [0, 128]]), ALU.mult, rd=[("ps", b), "dec"], wr=["kvd"])
                    P.op("dve", lambda e: e.tensor_tensor_scan(out=stt_[:, :, :].rearrange("p e c -> p (e c)"),
                                                               data0=decs[:, :, :].rearrange("p e c -> p (e c)"),
                                                               data1=kvd[:, :, :].rearrange("p e c -> p (e c)"), initial=0.0,
                                                               op0=ALU.mult, op1=ALU.add),
                         rd=["decs", "kvd"], wr=["stt"])
                    self.act(Sball[:, 1:NTILE, :], stt_[:, :, 0:NTILE - 1].rearrange("p e c -> p c e"), AF.Copy, rd=["stt"], wr=["Sball"])
                    for g0 in range(0, NTILE, 4):
                        gn = min(4, NTILE - g0)
                        b = 4 + (g0 // 4) % 2
                        for i in range(gn):
                            c = g0 + i
                            cs = slice(c * 128, (c + 1) * 128)
                            first = pos[c] == 0
                            if not first:
                                self.mm(self.ps[b][:, i * 128:(i + 1) * 128], Sball[:, pos[c], :], qt[:, cs], True, False,
                                        rd=["Sball", "qt"], wr=[("ps", b)])
                            self.mm(self.ps[b][:, i * 128:(i + 1) * 128], vh[:, c, :], atall[:, c, :], first, True,
                                    rd=[("vh", c), ("at", (c // 4) * 4)], wr=[("ps", b)])
                        osl = slice(g0 * 128, (g0 + gn) * 128)
                        if z == 0:
                            self.act(oT[:, osl], self.ps[b][:, 0:gn * 128], AF.Copy, rd=[("ps", b)], wr=[("oT", g0)])
                        else:
                            self.tt("dve", oT[:, osl], oT[:, osl], self.ps[b][:, 0:gn * 128], ALU.add,
                                    rd=[("ps", b), ("oT", g0)], wr=[("oT", g0)])
                for bi, (t0, n) in enumerate(TB):
                    if bi == 0 and not upd_ctx:
                        continue
                    okeys = [("oT", g0) for g0 in range(0, NTILE, 4)]
                    sq, sqk = fr.next()
                    self.act(sq[:, 0:n], oT[:, t0:t0 + n], AF.Square, rd=okeys, wr=[sqk])
                    self.mm(self.ps[0][:, 0:n], self.ones32[:, :], sq[:, 0:n], True, True, rd=["ones32", sqk], wr=[("ps", 0)])
                    rs, rsk = fr.next()
                    self.rstd_from(rs[:, 0:n], self.ps[0][:, 0:n], 128, rd=[("ps", 0)], wr=[rsk])
                    self.stt(rs[:, 0:n], oT[:, t0:t0 + n], self.evgl[:, j, h:h + 1], rs[:, 0:n], ALU.mult, ALU.mult,
                             rd=okeys + [rsk, "ev_gla_normT"], wr=[rsk])
                    self.tt("dve", self.bT[:, h, t0:t0 + n], rs[:, 0:n], szb[:, t0:t0 + n], ALU.mult, rd=[rsk, ("szb", bi)],
                            wr=[("bT", bi, h)])
            P.barrier()

    def wout_phase(self, l, wsrc, src, last):
        P, D = self.P, self.D
        upd_ctx = l < 3
        with ExitStack() as st:
            wo = self.sb(st, "wo", [128, 8, D_MODEL], BF16)
            xrot = self.rot(st, "xo", [128, 512], F32, 6)
            for hh in range(2):
                P.dma(wo[:, :, hh * 512:(hh + 1) * 512], wsrc[:, hh * 512:(hh + 1) * 512].rearrange("(k p) n -> p k n", p=128),
                      wr=[("wo", hh)], q="pool")
            for bi, (t0, n) in enumerate(TB):
                if bi == 0 and not upd_ctx:
                    continue
                ci = 1 if bi == 0 else 0
                for fo in range(8):
                    b = (bi * 8 + fo) % 6
                    pb = self.ps[b]
                    for k in range(8):
                        opnd = self.aT[:, k, t0:t0 + n] if k < 4 else self.bT[:, k - 4, t0:t0 + n]
                        self.mm(pb[:, 0:n], wo[:, k, fo * 128:(fo + 1) * 128], opnd, k == 0, k == 7,
                                rd=[("wo", fo // 4), ("actT", bi)], wr=[("ps", b)])
                    xt, xk = xrot.next()
                    P.dma(xt[:, 0:n], src[fo * 128:(fo + 1) * 128, t0:t0 + n], rd=[("xres", bi, fo)], wr=[xk])
                    self.stt(xt[:, 0:n], pb[:, 0:n], self.mod[:, l, 16 + fo, ci:ci + 1], xt[:, 0:n], ALU.mult, ALU.add,
                             rd=[("ps", b), xk, ("mod", l)], wr=[xk])
                    if not last:
                        P.dma(D["xr"][fo * 128:(fo + 1) * 128, t0:t0 + n], xt[:, 0:n], rd=[xk], wr=[("xres", bi, fo), ("xres", bi)])
                    if last and bi > 0:
                        P.dma(D["outT"][fo * 128:(fo + 1) * 128, t0 - NCTX:t0 - NCTX + n], xt[:, 0:n], rd=[xk], wr=[("outT", bi, fo)],
                              is_output=True)
                    if self.debug_full:
                        P.dma(D["dbgT"][fo * 128:(fo + 1) * 128, t0:t0 + n], xt[:, 0:n], rd=[xk], wr=[("dbgT", bi, fo)],
                              is_output=True)
            P.barrier()

    def actT_done(self):
        pass

    def odd_mixer(self, l):
        P, D = self.P, self.D
        j = l // 2
        upd_ctx = l < 3
        win = D["od_w_in"][j]
        rpb = D["od_rpbF"][j]
        OFF = 10
        AT = {0: range(0, 6), 1: range(2, 10), 2: range(6, 14), 3: range(10, 16)}
        with ExitStack() as st:
            wpr = self.rot(st, "wp", [128, 8, 4, 128], BF16, 2)
            qTp = None
            qTz = self.sb(st, "qTz", [128, 2, NT], BF16)
            P.op("pool", lambda e: e.memset(qTz[:], 0.0), wr=["qTz0"])
            kTp = self.sb(st, "kTp", [128, NT], BF16)
            Vp = self.sb(st, "Vp", [128, NTILE, 2, 65], BF16)
            szp = self.sb(st, "szp", [128, NT], F32)
            stg = self.sb(st, "tstg", [128, 15, 64], F32)
            TabFs = [self.sb(st, "TabF%d" % i, [128, 22, 64], F32) for i in range(2)]
            TabIs = [self.sb(st, "TabI%d" % i, [128, 22, 64], F32) for i in range(2)]
            cmask = self.sb(st, "cmask", [128, 64], F32)
            fr = self.rot(st, "f", [128, 512], F32, 8)
            er = self.rot(st, "e", [128, 512], F32, 6)
            ptr = self.rot(st, "pt", [128, 512], BF16, 8)
            P.dma(cmask[:], D["colmask"], wr=["cmask"])
            P.op("pool", lambda e: e.memset(Vp[:, :, :, 64:65], 1.0), wr=[("V1",)])
            for i in range(2):
                P.op("pool", lambda e, i=i: e.memset(TabIs[i][:], 0.0), wr=[("TabI0", i)])
                P.op("pool", lambda e, i=i: e.memset(TabFs[i][:], 0.0), wr=[("TabF", i, 0), ("TabF", i, 1)])
            winv = win.rearrange("(k p) (s c) -> p k s c", p=128, s=4)
            for pr in range(8):
                wp, wpk = wpr.next()
                for s_ in range(4):
                    P.dma(wp[:, :, s_, :], winv[:, :, s_, pr * 128:(pr + 1) * 128], wr=[(wpk, s_)], q="pool")
                for bi, (t0, n) in enumerate(TB):
                  def qkchain(s_, dstT, gcol, bi=bi, t0=t0, n=n):
                        bA = s_ + 4 * (bi % 2)
                        bB = 2 + s_ + 4 * (bi % 2)
                        pb, pk = self.ps[bA], ("ps", bA)
                        if bB == 7:
                            pB, pBk = self.psT[:, :].bitcast(F32), "psT"
                        else:
                            pB, pBk = self.ps[bB], ("ps", bB)
                        self.proj(pb[:, 0:n], lambda k, s_=s_: wp[:, k, s_, :], bi, pk, [(wpk, s_)])
                        yield
                        sq, sqk = fr.next()
                        self.act(sq[:, 0:n], pb[:, 0:n], AF.Square, rd=[pk], wr=[sqk])
                        yield
                        self.mm(pB[:, 0:n], self.bd64[:, :], sq[:, 0:n], True, True, rd=["bd64", sqk], wr=[pBk])
                        yield
                        f_, fk = fr.next()
                        self.rstd_from(f_[:, 0:n], pB[:, 0:n], 64, rd=[pBk], wr=[fk])
                        yield
                        if s_ == 1:
                            self.stt(dstT[:, t0:t0 + n], pb[:, 0:n], self.odg[:, j, gcol:gcol + 1], f_[:, 0:n], ALU.mult, ALU.mult,
                                     rd=[pk, fk, "od_gains"], wr=[("qk", s_, bi)])
                        else:
                            for hq in range(2):
                                r_ = slice(hq * 64, (hq + 1) * 64)
                                self.stt(qTz[r_, hq, t0:t0 + n], pb[r_, 0:n], self.odg[r_, j, gcol:gcol + 1], f_[r_, 0:n],
                                         ALU.mult, ALU.mult, rd=[pk, fk, "od_gains", "qTz0"], wr=[("qk", s_, bi, hq)])
                        yield
                  lockstep([qkchain(0, qTp, 0), qkchain(1, kTp, 1)])
                for bi, (t0, n) in enumerate(TB):
                    pz, pzk = self.ps[4 + bi % 2], ("ps", 4 + bi % 2)
                    self.proj(pz[:, 0:n], lambda k: wp[:, k, 3, :], bi, pzk, [(wpk, 3)])
                    self.act(szp[:, t0:t0 + n], pz[:, 0:n], AF.Silu, rd=[pzk], wr=[("szp", bi, 0), ("szp", bi, 1)])
                for tt_ in range(NTILE):
                    b = 4 + tt_ % 3
                    for k in range(8):
                        self.mm(self.ps[b][:, 0:128], self.hT[:, k, tt_ * 128:(tt_ + 1) * 128], wp[:, k, 2, :], k == 0, k == 7,
                                rd=[("hT", blk_of_tile(tt_)), (wpk, 2)], wr=[("ps", b)])
                    self.act(Vp[:, tt_, :, 0:64], self.ps[b][:, 0:128].rearrange("p (h e) -> p h e", e=64), AF.Copy,
                             rd=[("ps", b)], wr=[("Vp", tt_)])
                tabkeys = {}
                for hh in range(2):
                    h = 2 * pr + hh
                    TabF, TabI = TabFs[hh], TabIs[hh]
                    for half in range(2):
                        src = bass.AP(rpb.tensor, rpb.offset + 64 + h * 465 - 48, [[1, 64], [31, 15], [1, 64]])
                        P.dma(stg[half * 64:(half + 1) * 64, :, :], src, wr=[("tstg", half)])
                    self.act(stg[:, :, :], stg[:, :, :], AF.Exp, rd=[("tstg", 0), ("tstg", 1)], wr=[("tstg", 0), ("tstg", 1)])
                    for half in range(2):
                        p0 = half * 64
                        rv = view(stg[p0:p0 + 64, 0:1, 63:64], [[64, 15], [-1, 64]])
                        cm = view(cmask[p0:p0 + 64, 0:1], [[0, 15], [1, 64]])
                        self.tt("dve", TabF[p0:p0 + 64, 3 + half:18 + half, :], rv, cm, ALU.mult,
                                rd=[("tstg", half), "cmask"], wr=[("TabF", hh, half)])
                        P.op("dve", lambda e, p0=p0, half=half, TabF=TabF, TabI=TabI: e.tensor_copy(
                            out=TabI[p0:p0 + 64, 7 + half:15 + half, :], in_=TabF[p0:p0 + 64, 7 + half:15 + half, :]),
                             rd=[("TabF", hh, half), ("TabI0", hh)], wr=[("TabI", hh, half)])
                    tabkeys[hh] = [("TabF", hh, 0), ("TabF", hh, 1), ("TabI", hh, 0), ("TabI", hh, 1)]
                dstT = self.aT if pr < 4 else self.bT
                for bi, (t0, n) in enumerate(TB):
                    if bi == 0 and not upd_ctx:
                        continue
                    streams = []
                    if bi == 0:
                        keys = [0, 1]
                    else:
                        gq = bi - 1
                        r0 = 8 * gq
                        keys = [0, 1] + [2 + a for a in AT[gq]]
                    for hh in range(2):
                        post = None
                        if bi > 0:
                            def post(i, kt, psb, pskey, gq=gq, r0=r0, n=n, hh=hh):
                                TabF, TabI = TabFs[hh], TabIs[hh]
                                pt, ptk = ptr.next()
                                if kt < 2:
                                    self.act(pt[:, 0:n], psb[:, 0:n], AF.Exp, rd=[pskey], wr=[ptk], scale=C_SCALE)
                                    return pt, ptk
                                a = kt - 2
                                e_, ek = er.next()
                                self.act(e_[:, 0:n], psb[:, 0:n], AF.Exp, rd=[pskey], wr=[ek], scale=C_SCALE)
                                i0 = r0 - 2 * a + OFF
                                if gq == 0 and a <= 3:
                                    parts = [(0, 4, TabF), (4, 8, TabI)]
                                elif gq == 3 and a >= 12:
                                    parts = [(0, 5, TabI), (5, 8, TabF)]
                                else:
                                    parts = [(0, 8, TabI)]
                                for (ra, rb_, tab) in parts:
                                    self.tt("dve", pt[:, ra * 64:rb_ * 64], e_[:, ra * 64:rb_ * 64],
                                            tab[:, i0 + ra:i0 + rb_, :].rearrange("p i c -> p (i c)"), ALU.mult,
                                            rd=[ek] + tabkeys[hh], wr=[ptk])
                                return pt, ptk
                        streams.append(dict(
                            po=self.ps[4 + hh], bo=4 + hh,
                            kfn=lambda kt: kTp[:, kt * 128:(kt + 1) * 128],
                            kkeys=lambda kt: [("qk", 1, blk_of_tile(kt))],
                            q_ap=qTz[:, hh, t0:t0 + n], qkeys=[("qk", 0, bi, hh)],
                            vfn=lambda kt, hh=hh: Vp[:, kt, hh, :], vkeys=lambda kt: [("Vp", kt), ("V1",)], post=post))
                    self.attn_multi(streams, keys, n, C_SCALE, ptr)
                    for hh in range(2):
                        h = 2 * pr + hh
                        dst = dstT[hh * 64:(hh + 1) * 64, pr % 4, t0:t0 + n]
                        self.attn_finish(self.ps[4 + hh], 4 + hh, n, szp[hh * 64:(hh + 1) * 64, t0:t0 + n], ("szp", bi, hh), dst,
                                         [("oT", bi, h)], fr, bbc=hh)
            P.barrier()


def _partner():
    p = np.arange(32)
    half = (p // 8) % 2
    return np.where(half == 0, p + 8, p - 8)


def _rope_tables():
    t = np.arange(NLAT)
    row = (t // 64).astype(np.float32)
    col = (t % 64).astype(np.float32)
    inv = (np.float32(10000.0) ** (-np.arange(0, 16, 2, dtype=np.float32) / np.float32(16))).astype(np.float32)
    tab = np.zeros((2, 128, NLAT), np.float32)
    for d in range(32):
        a, half, f = d // 16, (d // 8) % 2, d % 8
        ang = ((row if a == 0 else col) * inv[f]).astype(np.float32)
        tab[0, 64 + d] = np.cos(ang)
        tab[1, 64 + d] = np.sin(ang) * (-1.0 if half == 0 else 1.0)
    return tab


def _consts():
    m = np.arange(128)
    tri = np.zeros((128, 2, 128), np.float32)
    tri[:, 0, :] = (m[:, None] <= m[None, :])
    tri[:, 1, :] = (m[:, None] >= m[None, :])
    c = np.arange(64)
    cs = np.clip(c - 8, 0, 48)
    kc = np.arange(64)
    cm = ((kc[:, None] >= cs[None, :]) & (kc[:, None] < cs[None, :] + 16)).astype(np.float32)
    colmask = np.concatenate([cm, cm], axis=0)
    bd = np.zeros((128, 128), np.float32)
    bd[:64, :64] = 1
    bd[64:, 64:] = 1
    return dict(rope=_rope_tables(), tri=tri, colmask=np.ascontiguousarray(colmask), bd64=bd)


def prep_inputs(inp):
    f = lambda a: np.ascontiguousarray(np.asarray(a, dtype=np.float32))
    pt = _partner()
    sh = {}
    sh["norm_gT"] = f(inp["norm_g"].reshape(4, 8, 128).transpose(2, 0, 1))
    sh["ada_w"] = f(inp["ada_w"])
    sh["ada_bT"] = f(inp["ada_b"].reshape(4, 24, 128).transpose(2, 0, 1))
    w_in = np.asarray(inp["ev_w_in"])
    sh["ev_w_in"] = f(np.concatenate([w_in, w_in[:, :, 1024 + pt]], axis=2))
    sh["ev_q_normT"] = f(inp["ev_q_norm"].reshape(2, 6, 128).transpose(2, 0, 1))
    w_uq = np.asarray(inp["ev_w_uq"])
    rot_cols = np.concatenate([h * 96 + 64 + pt for h in range(8)])
    sh["ev_w_uq"] = f(np.concatenate([w_uq, w_uq[:, :, rot_cols]], axis=2))
    sh["ev_kv_normT"] = f(inp["ev_kv_norm"].reshape(2, 2, 128).transpose(2, 0, 1))
    sh["ev_w_ukv"] = f(inp["ev_w_ukv"])
    g = np.zeros((128, 2, 4), np.float32)
    qg, kg = np.asarray(inp["ev_q_gain"]), np.asarray(inp["ev_k_gain"])
    for j in range(2):
        g[0:96, j, 0] = qg[j]
        g[64:96, j, 1] = qg[j, 64 + pt]
        g[0:96, j, 2] = kg[j]
        g[64:96, j, 3] = kg[j, 64 + pt]
    sh["ev_gains"] = g
    sh["ev_gate_w"] = f(inp["ev_gate_w"])
    sh["ev_gate_bT"] = f(inp["ev_gate_b"].reshape(2, 2, 4, 64).transpose(3, 0, 1, 2).reshape(64, 2, 8))
    sh["ev_gla_normT"] = f(inp["ev_gla_norm"].reshape(2, 4, 128).transpose(2, 0, 1))
    sh["ev_w_out"] = f(inp["ev_w_out"])
    sh["od_w_in"] = f(inp["od_w_in"])
    og = np.zeros((128, 2, 2), np.float32)
    for j in range(2):
        og[:, j, 0] = np.tile(np.asarray(inp["od_q_gain"])[j], 2)
        og[:, j, 1] = np.tile(np.asarray(inp["od_k_gain"])[j], 2)
    sh["od_gains"] = og
    rpbf = np.asarray(inp["od_rpb"])[:, :, ::-1, :].reshape(2, -1)
    sh["od_rpbF"] = f(np.concatenate([np.zeros((2, 64), np.float32), rpbf, np.zeros((2, 64), np.float32)], axis=1))
    sh["od_w_out"] = f(inp["od_w_out"])
    sh.update(_consts())
    x, ctx, c, c_ctx = (np.asarray(inp[k]) for k in ("x", "ctx", "c", "c_ctx"))
    per = []
    for b in range(x.shape[0]):
        d = dict(sh)
        d["xT"] = f(np.concatenate([ctx[b], x[b]], axis=0).T)
        c2 = np.stack([c[b].reshape(8, 128).T, c_ctx.reshape(8, 128).T], axis=2)
        d["c2"] = f(c2)
        per.append(d)
    return per


_NC_CACHE = {}


def kernel(**inputs):
    per = prep_inputs(inputs)
    key = "full"
    if key not in _NC_CACHE:
        _NC_CACHE[key] = Builder([0, 1, 2, 3]).build()
    nc = _NC_CACHE[key]
    res = run_bass_kernel_spmd(nc, per, core_ids=list(range(len(per))))
    out = np.stack([np.ascontiguousarray(r["outT"].T) for r in res.results], axis=0)
    return out.astype(np.float32)
```
